# Optimizing a Trainium2 kernel written in Bass

```python
import jax, jax.numpy as jnp
from jax import lax
import numpy as np

D_MODEL = 2048
BATCH = 16
SEQ = 2048
DEPTH = 4

HEAD_DIM = 128
N_HEADS_NA = 6
N_HEADS_DIL = 6
N_HEADS_MEM = 4
W_NA = N_HEADS_NA * HEAD_DIM
W_DIL = N_HEADS_DIL * HEAD_DIM
W_MEM = N_HEADS_MEM * HEAD_DIM
MIX_WIDTH = W_NA + W_DIL + W_MEM
IN_SPLITS = (W_NA,) * 4 + (W_DIL,) * 4 + (W_MEM,) * 2
IN_COLS = sum(IN_SPLITS)
N_MEM = 256
GRID_W = 64
NA_WIN_ROWS = 8
NA_WIN_COLS = 16
NA_QCOL_BLOCK = 16
NA_KCOL_BLOCK = 32
DIL_CONFIGS = ((128, 1), (512, 4), (2048, 16))
ROPE_THETA = 10000.0
EPS = 1e-6
NEG = -1e30
SCALE = HEAD_DIM ** -0.5

kernel_name = "hybrid_natten_dilated_memory_encoder"


def rms_norm(x, g):
    xf = x.astype(jnp.float32)
    y = xf * lax.rsqrt(jnp.mean(xf * xf, axis=-1, keepdims=True) + EPS)
    return (y * g.astype(jnp.float32)).astype(x.dtype)


def rope(x, pos):
    half = HEAD_DIM // 2
    inv = ROPE_THETA ** (-jnp.arange(half, dtype=jnp.float32) / half)
    ang = pos.astype(jnp.float32)[:, None] * inv[None, :]
    cos = jnp.cos(ang)[None, :, None, :]
    sin = jnp.sin(ang)[None, :, None, :]
    xf = x.astype(jnp.float32)
    x1, x2 = xf[..., :half], xf[..., half:]
    return jnp.concatenate([x1 * cos - x2 * sin, x2 * cos + x1 * sin], axis=-1).astype(x.dtype)


def neighbourhood_attention(q, k, v, rpb):
    B, T, H, hd = q.shape
    rows = T // GRID_W
    win_r = min(NA_WIN_ROWS, rows)
    n_cb = GRID_W // NA_QCOL_BLOCK

    def grid(a):
        return a.reshape(B, rows, GRID_W, H, hd).transpose(1, 0, 3, 2, 4)

    qg, kg, vg = grid(q), grid(k), grid(v)
    qcol = np.arange(GRID_W).reshape(n_cb, NA_QCOL_BLOCK)
    kstart = np.clip(np.arange(n_cb) * NA_QCOL_BLOCK - NA_WIN_COLS // 2, 0, GRID_W - NA_KCOL_BLOCK)
    kcol = kstart[:, None] + np.arange(NA_KCOL_BLOCK)[None, :]
    cstart = np.clip(qcol - NA_WIN_COLS // 2, 0, GRID_W - NA_WIN_COLS)
    col_ok = (kcol[:, None, :] >= cstart[:, :, None]) & (kcol[:, None, :] < cstart[:, :, None] + NA_WIN_COLS)
    dc_idx = np.clip(kcol[:, None, :] - qcol[:, :, None], -(NA_WIN_COLS - 1), NA_WIN_COLS - 1) + NA_WIN_COLS - 1
    rpb32 = rpb.astype(jnp.float32)

    def one_row(r):
        rs = jnp.clip(r - win_r // 2, 0, rows - win_r)
        k_rows = lax.dynamic_slice_in_dim(kg, rs, win_r, axis=0)
        v_rows = lax.dynamic_slice_in_dim(vg, rs, win_r, axis=0)
        k_blk = k_rows[:, :, :, kcol]
        v_blk = v_rows[:, :, :, kcol]
        q_blk = qg[r].reshape(B, H, n_cb, NA_QCOL_BLOCK, hd)
        s = jnp.einsum('bhcqd,wbhcjd->bhcqwj', q_blk, k_blk).astype(jnp.float32) * SCALE
        dr_idx = rs + jnp.arange(win_r) - r + NA_WIN_ROWS - 1
        bias = rpb32[:, dr_idx[None, None, :, None], dc_idx[:, :, None, :]]
        s = jnp.where(col_ok[:, :, None, :], s + bias[None], NEG)
        p = jax.nn.softmax(s.reshape(B, H, n_cb, NA_QCOL_BLOCK, win_r * NA_KCOL_BLOCK), axis=-1)
        p = p.reshape(s.shape).astype(v.dtype)
        o = jnp.einsum('bhcqwj,wbhcjd->bhcqd', p, v_blk)
        return o.reshape(B, H, GRID_W, hd)

    out = lax.map(one_row, jnp.arange(rows))
    return out.transpose(1, 0, 3, 2, 4).reshape(B, T, H, hd)


def banded_attention(q, k, v, half):
    lead = q.shape[:-2]
    L, hd = q.shape[-2], q.shape[-1]
    nb = -(-L // half)
    Lp = nb * half
    nl = len(lead)
    qb = jnp.pad(q, [(0, 0)] * nl + [(0, Lp - L), (0, 0)]).reshape(*lead, nb, half, hd)

    def key_blocks(a):
        c = jnp.pad(a, [(0, 0)] * nl + [(half, Lp - L + half), (0, 0)]).reshape(*lead, nb + 2, half, hd)
        return jnp.concatenate([c[..., :-2, :, :], c[..., 1:-1, :, :], c[..., 2:, :, :]], axis=-2)

    kb, vb = key_blocks(k), key_blocks(v)
    qpos = np.arange(Lp).reshape(nb, half)
    kpos = (np.arange(nb)[:, None] - 1) * half + np.arange(3 * half)[None, :]
    ok = (np.abs(kpos[:, None, :] - qpos[:, :, None]) <= half) & (kpos[:, None, :] >= 0) & (kpos[:, None, :] < L)
    s = jnp.einsum('...nqd,...nkd->...nqk', qb, kb).astype(jnp.float32) * SCALE
    s = jnp.where(ok, s, NEG)
    lse = jax.nn.logsumexp(s, axis=-1)
    p = jnp.exp(s - lse[..., None]).astype(v.dtype)
    o = jnp.einsum('...nqk,...nkd->...nqd', p, vb)
    return o.reshape(*lead, Lp, hd)[..., :L, :], lse.reshape(*lead, Lp)[..., :L]


def dilated_attention(q, k, v):
    B, T, H, hd = q.shape
    outs, lses = [], []
    for window, dil in DIL_CONFIGS:
        half = (window // 2) // dil
        L = T // dil

        def stream(a):
            return a.reshape(B, L, dil, H, hd).transpose(0, 2, 3, 1, 4)

        o, lse = banded_attention(stream(q), stream(k), stream(v), half)
        outs.append(o.transpose(0, 3, 1, 2, 4).reshape(B, T, H, hd))
        lses.append(lse.transpose(0, 3, 1, 2).reshape(B, T, H))
    w = jax.nn.softmax(jnp.stack(lses), axis=0)
    out = jnp.sum(w[..., None] * jnp.stack(outs).astype(jnp.float32), axis=0)
    return out.astype(q.dtype)


def memory_attention(q, mk, mv):
    s = jnp.einsum('bthd,bmhd->bhtm', q, mk).astype(jnp.float32) * SCALE
    p = jax.nn.softmax(s, axis=-1).astype(mv.dtype)
    return jnp.einsum('bhtm,bmhd->bthd', p, mv)


def setup_inputs(seed: int = 0) -> dict:
    key = jax.random.key(seed)
    ks = jax.random.split(key, 10)
    f32 = jnp.float32
    x = jax.random.normal(ks[0], (BATCH, SEQ, D_MODEL), f32)
    mem = jax.random.normal(ks[1], (BATCH, N_MEM, D_MODEL), f32)
    norm_g = 1.0 + 0.02 * jax.random.normal(ks[2], (DEPTH, D_MODEL), f32)
    w_in = jax.random.normal(ks[3], (DEPTH, D_MODEL, IN_COLS), f32) * D_MODEL ** -0.5
    na_rpb = 0.02 * jax.random.normal(ks[4], (DEPTH, N_HEADS_NA, 2 * NA_WIN_ROWS - 1, 2 * NA_WIN_COLS - 1), f32)
    mem_norm_g = 1.0 + 0.02 * jax.random.normal(ks[5], (D_MODEL,), f32)
    w_mem_kv = jax.random.normal(ks[6], (DEPTH, D_MODEL, 2 * W_MEM), f32) * D_MODEL ** -0.5
    w_out = jax.random.normal(ks[7], (DEPTH, MIX_WIDTH, D_MODEL), f32) * MIX_WIDTH ** -0.5
    final_g = 1.0 + 0.02 * jax.random.normal(ks[8], (D_MODEL,), f32)
    return {"x": x, "mem": mem, "norm_g": norm_g, "w_in": w_in, "na_rpb": na_rpb,
            "mem_norm_g": mem_norm_g, "w_mem_kv": w_mem_kv, "w_out": w_out, "final_g": final_g}


def reference(x, mem, norm_g, w_in, na_rpb, mem_norm_g, w_mem_kv, w_out, final_g):
    B, T, _ = x.shape
    pos = jnp.arange(T)
    offsets = np.cumsum(IN_SPLITS)[:-1].tolist()
    mem_n = rms_norm(mem, mem_norm_g)

    def heads(a, n):
        return a.reshape(a.shape[0], a.shape[1], n, HEAD_DIM)

    for l in range(DEPTH):
        h = rms_norm(x, norm_g[l])
        z = h @ w_in[l]
        na_q, na_k, na_v, na_g, dl_q, dl_k, dl_v, dl_g, m_q, m_g = jnp.split(z, offsets, axis=-1)
        o_na = neighbourhood_attention(heads(na_q, N_HEADS_NA), heads(na_k, N_HEADS_NA),
                                       heads(na_v, N_HEADS_NA), na_rpb[l]).reshape(B, T, W_NA)
        o_dl = dilated_attention(rope(heads(dl_q, N_HEADS_DIL), pos), rope(heads(dl_k, N_HEADS_DIL), pos),
                                 heads(dl_v, N_HEADS_DIL)).reshape(B, T, W_DIL)
        mk, mv = jnp.split(mem_n @ w_mem_kv[l], 2, axis=-1)
        o_m = memory_attention(heads(m_q, N_HEADS_MEM), heads(mk, N_HEADS_MEM),
                               heads(mv, N_HEADS_MEM)).reshape(B, T, W_MEM)
        y = jnp.concatenate([o_na * jax.nn.silu(na_g), o_dl * jax.nn.silu(dl_g), o_m * jax.nn.silu(m_g)], axis=-1)
        x = x + y @ w_out[l]
    return rms_norm(x, final_g)
```

```python
import contextlib
import numpy as np
import concourse.bass as bass
import concourse.mybir as mybir
from concourse.bass_utils import run_bass_kernel_spmd

F32 = mybir.dt.float32
BF16 = mybir.dt.bfloat16
AF = mybir.ActivationFunctionType
ALU = mybir.AluOpType

N_CORES = 8
D = 2048
T = 2048
NSEQ = 2
DEPTH = 4
HD = 128
NCH = 16
IN_COLS = 7168
N_MEM = 256
SCALE = float(HD ** -0.5)
EPS = 1e-6
NEG = -30000.0
SEM_LIMIT = 60000
W_OFF = 1408
W_LEN = 2944

LAYERS_PER_LAUNCH = 4


def _rs(r):
    return min(max(r - 4, 0), 24)


def _cs(c):
    return min(max(c - 8, 0), 48)


def na_plan():
    types = {}
    type_list = []
    plan = []
    for qi in range(16):
        lst = []
        for j in range(16):
            valid = []
            for krl in range(2):
                for qrl in range(2):
                    kr = 2 * j + krl
                    qr = 2 * qi + qrl
                    valid.append(_rs(qr) <= kr < _rs(qr) + 8)
            if not any(valid):
                continue
            key = (j - qi, tuple(valid))
            if key not in types:
                types[key] = len(type_list)
                type_list.append((qi, j))
            lst.append((j, types[key]))
        plan.append(lst)
    return plan, type_list


def na_bias_tables(na_rpb):
    plan, type_list = na_plan()
    nt = len(type_list)
    dr_idx = np.zeros((nt, 128, 128), np.int64)
    dc_idx = np.zeros((nt, 128, 128), np.int64)
    ok = np.zeros((nt, 128, 128), bool)
    p = np.arange(128)
    krl, kc = p // 64, p % 64
    qrl, qc = p // 64, p % 64
    cs = np.clip(qc - 8, 0, 48)
    for t, (qi, j) in enumerate(type_list):
        kr = 2 * j + krl[:, None]
        qr = 2 * qi + qrl[None, :]
        rs = np.clip(qr - 4, 0, 24)
        row_ok = (kr >= rs) & (kr < rs + 8)
        col_ok = (kc[:, None] >= cs[None, :]) & (kc[:, None] < cs[None, :] + 16)
        ok[t] = row_ok & col_ok
        dr_idx[t] = np.clip(kr - qr + 7, 0, 14)
        dc_idx[t] = np.clip(kc[:, None] - qc[None, :], -15, 15) + 15
    g = na_rpb[:, :, dr_idx, dc_idx]
    g = np.where(ok[None, None], g, np.float32(NEG)).astype(np.float32)
    g = np.ascontiguousarray(g.transpose(0, 1, 3, 2, 4)).reshape(DEPTH * 6 * 128, nt * 128)
    return g, plan, nt


def const_tables():
    half = HD // 2
    inv = (np.float32(10000.0) ** (-(np.arange(half, dtype=np.float32) / np.float32(half)))).astype(np.float32)
    ang = (np.arange(T, dtype=np.float32)[:, None] * inv[None, :]).astype(np.float32)
    cos = np.cos(ang.astype(np.float64)).astype(np.float32).T
    sin = np.sin(ang.astype(np.float64)).astype(np.float32).T
    cosT = np.concatenate([cos, cos], 0)
    sinS = np.concatenate([-sin, sin], 0)
    swap = np.zeros((128, 128), np.float32)
    for m in range(128):
        swap[(m + 64) % 128, m] = 1.0
    ident = np.eye(128, dtype=np.float32)
    ones = np.ones((128, 128), np.float32)
    p = np.arange(128)[:, None]
    u = np.arange(W_LEN)[None, :]
    d = p - u + W_OFF
    ad = np.abs(d)
    w = (ad <= 64).astype(np.float32) + ((d % 4 == 0) & (ad <= 256)).astype(np.float32) \
        + ((d % 16 == 0) & (ad <= 1024)).astype(np.float32)
    cst32 = np.concatenate([cosT, sinS, swap], 1).astype(np.float32)
    cst16 = np.concatenate([ident, ones, w], 1).astype(np.float32)
    return cst32, cst16


def _flat(deps):
    out = []
    for d in deps:
        if d is None:
            continue
        if isinstance(d, list):
            out.extend(_flat(d))
        else:
            out.append(d)
    return out


class Ctx:
    def __init__(self, nc, stack):
        self.nc = nc
        self.stack = stack
        self.nsem = 0

    def new_sem(self, name):
        self.nsem += 1
        return self.stack.enter_context(self.nc.semaphore(f"{name}_{self.nsem}"))


class Eng:
    def __init__(self, ctx, e, name):
        self.ctx = ctx
        self.e = e
        self.name = name
        self.sem = ctx.new_sem(name)
        self.n = 0
        self.seen = {}
        self.last = None

    def wait(self, deps):
        for (sem, val) in _flat(deps):
            k = id(sem)
            if self.seen.get(k, 0) >= val:
                continue
            self.e.wait_ge(sem, val)
            self.seen[k] = val

    def op(self, ins, sig=True):
        if not sig:
            return None
        if self.n >= SEM_LIMIT:
            self.sem = self.ctx.new_sem(self.name)
            self.n = 0
        self.n += 1
        ins.then_inc(self.sem, 1)
        self.last = (self.sem, self.n)
        return self.last


class DSem:
    def __init__(self, ctx, name):
        self.ctx = ctx
        self.name = name
        self.sem = ctx.new_sem(name)
        self.n = 0

    def inc(self, ins):
        if self.n >= SEM_LIMIT:
            self.sem = self.ctx.new_sem(self.name)
            self.n = 0
        self.n += 16
        ins.then_inc(self.sem, 16)
        return (self.sem, self.n)


def build_program(layers, first, last, nab_cols, plan):
    nc = bass.Bass("TRN2", target_bir_lowering=False)
    NT128 = nab_cols
    x_in = nc.dram_tensor("x", [NSEQ * T, D], F32, kind="ExternalInput").ap()
    mem_in = nc.dram_tensor("mem", [NSEQ * N_MEM, D], F32, kind="ExternalInput").ap()
    gT_in = nc.dram_tensor("gT", [128, 96], F32, kind="ExternalInput").ap()
    fg_in = nc.dram_tensor("fgbc", [128, D], F32, kind="ExternalInput").ap()
    w_in = nc.dram_tensor("w_in", [DEPTH * D, IN_COLS], F32, kind="ExternalInput").ap()
    w_mem = nc.dram_tensor("w_mem", [DEPTH * D, 1024], F32, kind="ExternalInput").ap()
    w_out = nc.dram_tensor("w_out", [DEPTH * D, D], F32, kind="ExternalInput").ap()
    nab = nc.dram_tensor("nab", [DEPTH * 6 * 128, NT128], F32, kind="ExternalInput").ap()
    c32_in = nc.dram_tensor("cst32", [128, 2 * T + 128], F32, kind="ExternalInput").ap()
    c16_in = nc.dram_tensor("cst16", [128, 256 + W_LEN], F32, kind="ExternalInput").ap()
    y_out = nc.dram_tensor("y", [NSEQ * T, D], F32, kind="ExternalOutput").ap()
    xbuf = nc.dram_tensor("xbuf", [NSEQ * T, D], F32).ap()
    ybuf = nc.dram_tensor("ybuf", [2, D, T], BF16).ap()

    sb = nc.alloc_sbuf_tensor
    hT = sb("hT", [128, NCH, T], BF16)
    memnT = sb("memnT", [128, NCH, N_MEM], BF16)
    wbuf = sb("wbuf", [128, 8 * NCH * 128], BF16)
    cosT = sb("cosT", [128, T], F32)
    sinS = sb("sinS", [128, T], F32)
    swapm = sb("swapm", [128, 128], F32)
    c16 = sb("c16", [128, 256 + W_LEN], BF16)
    ident = c16[:, 0:128]
    ones = c16[:, 128:256]
    gT = sb("gTs", [128, 96], F32)
    stats = sb("stats", [128, 640], F32)
    U = sb("U", [128, 10 * T], BF16)
    qTb = [U[:, (0 + i) * T:(1 + i) * T] for i in range(2)]
    kTb = [U[:, (2 + i) * T:(3 + i) * T] for i in range(2)]
    sgb = [U[:, (4 + i) * T:(5 + i) * T] for i in range(2)]
    vb = [U[:, (6 + i) * T:(7 + i) * T] for i in range(2)]
    yTb = [U[:, (8 + i) * T:(9 + i) * T] for i in range(2)]
    xt = [U[:, (2 * i) * T:(2 * i + 2) * T].bitcast(F32) for i in range(2)]
    xs = [U[:, (4 + i) * T:(5 + i) * T] for i in range(2)]
    fgbc = U[:, 6 * T:8 * T].bitcast(F32)
    PTb = [sb(f"PT{i}", [128, 640], BF16) for i in range(4)]
    Eb = [sb(f"E{i}", [128, 512], BF16) for i in range(3)]
    Btb = [sb(f"Bt{i}", [128, NT128], BF16) for i in range(2)]
    f32tmp = [sb(f"ft{i}", [128, 512], F32) for i in range(8)]

    pjb = [nc.alloc_psum_tensor(f"pj{i}", [128, 512], F32) for i in range(2)]
    sbig = [nc.alloc_psum_tensor(f"sS{i}", [128, 1024], F32) for i in range(2)]
    odb = [nc.alloc_psum_tensor(f"od{i}", [128, 512], F32) for i in range(2)]

    with contextlib.ExitStack() as stack:
        ctx = Ctx(nc, stack)
        PE = Eng(ctx, nc.tensor, "pe")
        ACT = Eng(ctx, nc.scalar, "act")
        DVE = Eng(ctx, nc.vector, "dve")
        POOL = Eng(ctx, nc.gpsimd, "pool")
        SP = Eng(ctx, nc.sync, "sp")
        engines = [PE, ACT, DVE, POOL, SP]
        pending_dma = []

        def dma(eng, dsem, out, in_, deps=()):
            eng.wait(deps)
            tok = dsem.inc(eng.e.dma_start(out=out, in_=in_))
            pending_dma.append(tok)
            return tok

        def barrier():
            toks = [e.last for e in engines if e.last is not None] + list(pending_dma)
            for e in engines:
                e.wait(toks)
            del pending_dma[:]

        class Res:
            def __init__(self):
                self.readers = []
                self.writer = None

        cs_sem = DSem(ctx, "cst")
        t_c = [dma(SP, cs_sem, cosT[:], c32_in[:, 0:T]),
               dma(SP, cs_sem, sinS[:], c32_in[:, T:2 * T]),
               dma(SP, cs_sem, swapm[:], c32_in[:, 2 * T:2 * T + 128]),
               dma(SP, cs_sem, gT[:], gT_in[:, :]),
               dma(POOL, cs_sem, c16[:], c16_in[:, :])]
        t_c.append(DVE.op(nc.vector.memset(stats[:], 0.0)))
        barrier()

        stat_col = [0]
        xt_res = [Res(), Res()]
        xs_res = [Res(), Res()]
        pj_res = [Res(), Res()]
        xld = [DSem(ctx, "xld0"), DSem(ctx, "xld1")]
        xst = [DSem(ctx, "xst0"), DSem(ctx, "xst1")]
        cnt = {"xt": 0, "pj": 0, "w": 0, "z": 0, "y": 0, "S": 0, "S4": 0, "PT": 0, "E": 0, "od": 0, "ft": 0, "bt": 0}

        def rstd_chain(xtile, junk, deps):
            c = stat_col[0]
            stat_col[0] += 3
            assert stat_col[0] <= 640
            ACT.wait(deps)
            a1 = ACT.op(nc.scalar.activation(out=junk, in_=xtile, func=AF.Square, accum_out=stats[:, c:c + 1]))
            ACT.wait([a1])
            a2 = ACT.op(nc.scalar.activation(out=stats[:, c + 1:c + 2], in_=stats[:, c:c + 1], func=AF.Ln,
                                             scale=1.0 / D, bias=EPS))
            ACT.wait([a2])
            a3 = ACT.op(nc.scalar.activation(out=stats[:, c + 2:c + 3], in_=stats[:, c + 1:c + 2], func=AF.Exp,
                                             scale=-0.5))
            return a1, a3, stats[:, c + 2:c + 3]

        def phase_norm(src, r0, ntiles, gidx, dst):
            for tt in range(ntiles):
                k = cnt["xt"]
                cnt["xt"] += 1
                sl = k % 2
                tl = dma(SP, xld[sl], xt[sl], src[r0 + tt * 128:r0 + (tt + 1) * 128, :], deps=xt_res[sl].readers)
                a1, a3, rstd = rstd_chain(xt[sl], xs[sl], [tl] + xs_res[sl].readers)
                DVE.wait([a3])
                d1 = DVE.op(nc.vector.tensor_scalar(out=xs[sl], in0=xt[sl], scalar1=rstd, scalar2=None,
                                                    op0=ALU.mult))
                xt_res[sl].readers = [a1, d1]
                tps = []
                for half in range(2):
                    kb = cnt["pj"]
                    cnt["pj"] += 1
                    bank = pjb[kb % 2]
                    bres = pj_res[kb % 2]
                    pb = bank[:].bitcast(BF16)
                    PE.wait([d1] + bres.readers)
                    for c8 in range(8):
                        c = half * 8 + c8
                        ins = nc.tensor.transpose(out=pb[:, c8 * 128:(c8 + 1) * 128],
                                                  in_=xs[sl][:, c * 128:(c + 1) * 128], identity=ident)
                    tp = PE.op(ins)
                    tps.append(tp)
                    DVE.wait([tp])
                    gb = gT[:, gidx * 16 + half * 8:gidx * 16 + half * 8 + 8].unsqueeze(2).to_broadcast([128, 8, 128])
                    e1 = DVE.op(nc.vector.tensor_tensor(
                        out=dst[:, half * 8:(half + 1) * 8, tt * 128:(tt + 1) * 128],
                        in0=pb[:, 0:1024].rearrange("p (c t) -> p c t", c=8), in1=gb, op=ALU.mult))
                    bres.readers = [e1]
                xs_res[sl].readers = tps

        w_res = [Res() for _ in range(8)]
        w_sem = [DSem(ctx, f"w{i}") for i in range(8)]

        def wslot_view(s):
            return wbuf[:, s * 2048:(s + 1) * 2048].rearrange("p (c n) -> p c n", c=NCH)

        def load_wblock(wt, layer, c0):
            s = cnt["w"] % 8
            cnt["w"] += 1
            src = wt[layer * D:(layer + 1) * D, c0:c0 + 128].rearrange("(c p) n -> p c n", p=128)
            tok = dma(POOL, w_sem[s], wslot_view(s), src, deps=w_res[s].readers)
            w_res[s].writer = tok
            return s, tok

        z_res = [Res(), Res()]
        y_res = [Res(), Res()]
        yst = [DSem(ctx, "yst0"), DSem(ctx, "yst1")]
        S_res = [Res(), Res()]
        S4_res = [Res() for _ in range(4)]
        PT_res = [Res() for _ in range(4)]
        E_res = [Res() for _ in range(3)]
        od_res = [Res(), Res()]
        ft_res = [Res() for _ in range(8)]
        bt_res = [Res(), Res()]
        bt_sem = [DSem(ctx, "bt0"), DSem(ctx, "bt1")]

        def S4_view(i):
            return sbig[i // 2][:, (i % 2) * 512:(i % 2 + 1) * 512]

        def next_ft():
            k = cnt["ft"] % 8
            cnt["ft"] += 1
            return f32tmp[k], ft_res[k]

        def next_pj():
            k = cnt["pj"] % 2
            cnt["pj"] += 1
            return pjb[k], pj_res[k]

        def proj_fm(ws, wtok, rhs_of, ncols, evac, zdeps):
            wv = wslot_view(ws)
            nb_ = (T if ncols == 512 else N_MEM) // ncols
            mt = None
            for tb in range(nb_):
                bank, bres = next_pj()
                PE.wait([wtok] + bres.readers + zdeps)
                for c in range(NCH):
                    ins = nc.tensor.matmul(bank[:, 0:ncols], wv[:, c, :], rhs_of(c, tb), start=(c == 0),
                                           stop=(c == NCH - 1))
                mt = PE.op(ins)
                bres.readers = _flat([evac(tb, bank, mt)])
            w_res[ws].readers = [mt]

        def proj_tm(ws, wtok, lhs_of, ntiles, dst, zdeps):
            wv = wslot_view(ws)
            mt = None
            ngrp = (ntiles + 3) // 4
            for g in range(ngrp):
                bank, bres = next_pj()
                nt_ = min(4, ntiles - g * 4)
                PE.wait([wtok] + bres.readers + zdeps)
                for t4 in range(nt_):
                    tt = g * 4 + t4
                    for c in range(NCH):
                        ins = nc.tensor.matmul(bank[:, t4 * 128:(t4 + 1) * 128], lhs_of(c, tt), wv[:, c, :],
                                               start=(c == 0), stop=(c == NCH - 1))
                mt = PE.op(ins)
                ACT.wait([mt])
                ev = ACT.op(nc.scalar.copy(out=dst[:, g * 512:g * 512 + nt_ * 128], in_=bank[:, 0:nt_ * 128]))
                bres.readers = [ev]
            w_res[ws].readers = [mt]
            return ev

        def hT_rhs(c, tb):
            return hT[:, c, tb * 512:(tb + 1) * 512]

        def evac_copy(dst, scale):
            def f(tb, bank, tok):
                ACT.wait([tok])
                n = 512
                return ACT.op(nc.scalar.activation(out=dst[:, tb * n:(tb + 1) * n], in_=bank[:, 0:n], func=AF.Copy,
                                                   scale=scale))
            return f

        def evac_rope(dst, scale):
            def f(tb, bank, tok):
                qf, qres = next_ft()
                ACT.wait([tok] + qres.readers)
                a = ACT.op(nc.scalar.activation(out=qf[:], in_=bank[:, 0:512], func=AF.Copy, scale=scale))
                k = cnt["S4"] % 4
                cnt["S4"] += 1
                rp, rres = S4_view(k), S4_res[k]
                PE.wait([a] + rres.readers)
                r = PE.op(nc.tensor.matmul(rp, swapm[:], qf[:], start=True, stop=True))
                t1, t1res = next_ft()
                DVE.wait([a] + t1res.readers)
                d1 = DVE.op(nc.vector.tensor_tensor(out=t1[:], in0=qf[:], in1=cosT[:, tb * 512:(tb + 1) * 512],
                                                    op=ALU.mult))
                t2, t2res = next_ft()
                DVE.wait([r] + t2res.readers)
                d2 = DVE.op(nc.vector.tensor_tensor(out=t2[:], in0=rp, in1=sinS[:, tb * 512:(tb + 1) * 512],
                                                    op=ALU.mult))
                rres.readers = [d2]
                DVE.wait([d1, d2])
                d3 = DVE.op(nc.vector.tensor_tensor(out=dst[:, tb * 512:(tb + 1) * 512], in0=t1[:], in1=t2[:],
                                                    op=ALU.add))
                qres.readers = [r, d1]
                t1res.readers = [d3]
                t2res.readers = [d3]
                return a
            return f

        def evac_silu(dst):
            def f(tb, bank, tok):
                te, teres = next_ft()
                tg, tgres = next_ft()
                ACT.wait([tok] + teres.readers + tgres.readers)
                a1 = ACT.op(nc.scalar.activation(out=te[:], in_=bank[:, 0:512], func=AF.Exp, scale=-1.0))
                a2 = ACT.op(nc.scalar.copy(out=tg[:], in_=bank[:, 0:512]))
                DVE.wait([a1])
                d1 = DVE.op(nc.vector.tensor_scalar(out=te[:], in0=te[:], scalar1=1.0, scalar2=None, op0=ALU.add))
                DVE.wait([d1])
                d2 = DVE.op(nc.vector.reciprocal(out=te[:], in_=te[:]))
                DVE.wait([d2, a2])
                d3 = DVE.op(nc.vector.tensor_tensor(out=dst[:, tb * 512:(tb + 1) * 512], in0=tg[:], in1=te[:],
                                                    op=ALU.mult))
                teres.readers = [d3]
                tgres.readers = [d3]
                return [a1, a2]
            return f

        def normalize(o_ap, d_ap, n, ydst, sg_ap):
            rd, rres = next_ft()
            tq, tres = next_ft()
            DVE.wait(rres.readers)
            d1 = DVE.op(nc.vector.reciprocal(out=rd[:, 0:n], in_=d_ap))
            DVE.wait([d1] + tres.readers)
            d2 = DVE.op(nc.vector.tensor_tensor(out=tq[:, 0:n], in0=o_ap, in1=rd[:, 0:n], op=ALU.mult))
            DVE.wait([d2])
            d3 = DVE.op(nc.vector.tensor_tensor(out=ydst, in0=tq[:, 0:n], in1=sg_ap, op=ALU.mult))
            rres.readers = [d2]
            tres.readers = [d3]
            return d3, d2

        def store_y(yp, par, gh):
            tok = dma(SP, yst[yp], ybuf[par, gh * 128:(gh + 1) * 128, :], yTb[yp], deps=[DVE.last])
            y_res[yp].readers = [tok]

        def attn_na(zp, yp, btp, bt_tok):
            qT, kT, sg, v, yT = qTb[zp], kTb[zp], sgb[zp], vb[zp], yTb[yp]
            Bt = Btb[btp]
            prev = None
            last_pe = None

            def pv(st):
                qi, J, pk, et = st
                ko = cnt["od"] % 2
                cnt["od"] += 1
                od, ores = odb[ko], od_res[ko]
                PE.wait([et] + ores.readers)
                nj = len(J)
                for jj, (j, ty) in enumerate(J):
                    nc.tensor.matmul(od[:, 0:128], v[:, j * 128:(j + 1) * 128], PTb[pk][:, jj * 128:(jj + 1) * 128],
                                     start=(jj == 0), stop=(jj == nj - 1))
                for jj, (j, ty) in enumerate(J):
                    ins = nc.tensor.matmul(od[:, 128:256], ones, PTb[pk][:, jj * 128:(jj + 1) * 128],
                                           start=(jj == 0), stop=(jj == nj - 1))
                pt = PE.op(ins)
                PT_res[pk].readers = [pt]
                DVE.wait([pt])
                d3, d2 = normalize(od[:, 0:128], od[:, 128:256], 128, yT[:, qi * 128:(qi + 1) * 128],
                                   sg[:, qi * 128:(qi + 1) * 128])
                ores.readers = [d2]
                return pt

            for qi in range(16):
                J = plan[qi]
                ks = cnt["S"] % 2
                cnt["S"] += 1
                sbk, sres = sbig[ks], S_res[ks]
                PE.wait(sres.readers + [bt_tok])
                for jj, (j, ty) in enumerate(J):
                    nc.tensor.matmul(sbk[:, jj * 128:(jj + 1) * 128], kT[:, j * 128:(j + 1) * 128],
                                     qT[:, qi * 128:(qi + 1) * 128], start=True, stop=False)
                    ins = nc.tensor.matmul(sbk[:, jj * 128:(jj + 1) * 128], ident, Bt[:, ty * 128:(ty + 1) * 128],
                                           start=False, stop=True)
                st = PE.op(ins)
                pk = cnt["PT"] % 4
                cnt["PT"] += 1
                n = len(J) * 128
                ACT.wait([st] + PT_res[pk].readers)
                n1 = min(n, 512)
                et = ACT.op(nc.scalar.activation(out=PTb[pk][:, 0:n1], in_=sbk[:, 0:n1], func=AF.Exp))
                if n > 512:
                    et = ACT.op(nc.scalar.activation(out=PTb[pk][:, 512:n], in_=sbk[:, 512:n], func=AF.Exp))
                sres.readers = [et]
                if prev is not None:
                    last_pe = pv(prev)
                prev = (qi, J, pk, et)
            last_pe = pv(prev)
            z_res[zp].readers = [last_pe, DVE.last]
            bt_res[btp].readers = [last_pe]

        def attn_blocks(zp, yp, items, masked):
            qT, sg, yT = qTb[zp], sgb[zp], yTb[yp]
            LA = 2
            staged = []
            last_pe = [None]

            def pv(st):
                b, vap, pk, mtok, first, lastf = st
                if first:
                    PE.wait(od_res[0].readers + od_res[1].readers)
                PE.wait([mtok])
                nc.tensor.matmul(odb[0][:, :], vap, PTb[pk][:, 0:512], start=first, stop=lastf)
                pt = PE.op(nc.tensor.matmul(odb[1][:, :], ones, PTb[pk][:, 0:512], start=first, stop=lastf))
                PT_res[pk].readers = [pt]
                last_pe[0] = pt
                if lastf:
                    DVE.wait([pt])
                    d3, d2 = normalize(odb[0][:, :], odb[1][:, :], 512, yT[:, b * 512:(b + 1) * 512],
                                       sg[:, b * 512:(b + 1) * 512])
                    od_res[0].readers = [d2]
                    od_res[1].readers = [d2]

            for (b, kap, vap, moff, first, lastf) in items:
                k4 = cnt["S4"] % 4
                cnt["S4"] += 1
                sv_, sres = S4_view(k4), S4_res[k4]
                PE.wait(sres.readers)
                st = PE.op(nc.tensor.matmul(sv_, kap, qT[:, b * 512:(b + 1) * 512], start=True, stop=True))
                pk = cnt["PT"] % 4
                cnt["PT"] += 1
                if masked:
                    ke = cnt["E"] % 3
                    cnt["E"] += 1
                    ACT.wait([st] + E_res[ke].readers)
                    et = ACT.op(nc.scalar.activation(out=Eb[ke][:], in_=sv_, func=AF.Exp))
                    sres.readers = [et]
                    DVE.wait([et] + PT_res[pk].readers)
                    mtok = DVE.op(nc.vector.tensor_tensor(out=PTb[pk][:, 0:512], in0=Eb[ke][:],
                                                          in1=c16[:, 256 + moff:256 + moff + 512], op=ALU.mult))
                    E_res[ke].readers = [mtok]
                else:
                    ACT.wait([st] + PT_res[pk].readers)
                    mtok = ACT.op(nc.scalar.activation(out=PTb[pk][:, 0:512], in_=sv_, func=AF.Exp))
                    sres.readers = [mtok]
                staged.append((b, vap, pk, mtok, first, lastf))
                if len(staged) > LA:
                    pv(staged.pop(0))
            while staged:
                pv(staged.pop(0))
            z_res[zp].readers = [last_pe[0], DVE.last]

        def mem_norm(s):
            phase_norm(mem_in, s * N_MEM, 2, 5, memnT)

        def phase_B(s, l, par):
            heads = [("na", h) for h in range(6)] + [("dl", h) for h in range(6)] + [("mm", h) for h in range(4)]
            def wblocks(kind, h):
                if kind == "na":
                    return [(w_in, 0 + h * 128), (w_in, 768 + h * 128), (w_in, 1536 + h * 128), (w_in, 2304 + h * 128)]
                if kind == "dl":
                    return [(w_in, 3072 + h * 128), (w_in, 3840 + h * 128), (w_in, 4608 + h * 128),
                            (w_in, 5376 + h * 128)]
                return [(w_in, 6144 + h * 128), (w_mem, h * 128), (w_mem, 512 + h * 128), (w_in, 6656 + h * 128)]

            def issue_loads(idx):
                kind, h = heads[idx]
                res = {}
                if kind == "na":
                    kb = cnt["bt"] % 2
                    cnt["bt"] += 1
                    r0 = (l * 6 + h) * 128
                    res["bt"] = (kb, dma(POOL, bt_sem[kb], Btb[kb][:], nab[r0:r0 + 128, :], deps=bt_res[kb].readers))
                res["w"] = [load_wblock(wt, l, c0) for (wt, c0) in wblocks(kind, h)]
                return res

            loads = issue_loads(0)
            for hi, (kind, h) in enumerate(heads):
                cur = loads
                zp = cnt["z"] % 2
                cnt["z"] += 1
                yp = cnt["y"] % 2
                cnt["y"] += 1
                zdeps = list(z_res[zp].readers)
                (wq, tq), (wk, tk), (wvv, tv), (wg, tg) = cur["w"]
                rope = kind == "dl"
                proj_fm(wq, tq, hT_rhs, 512, (evac_rope if rope else evac_copy)(qTb[zp], SCALE), zdeps)
                if kind == "mm":
                    def ev_mk(tb, bank, tok):
                        ACT.wait([tok])
                        return ACT.op(nc.scalar.copy(out=kTb[zp][:, 0:N_MEM], in_=bank[:, 0:N_MEM]))
                    proj_fm(wk, tk, lambda c, tb: memnT[:, c, :], N_MEM, ev_mk, zdeps)
                else:
                    proj_fm(wk, tk, hT_rhs, 512, (evac_rope if rope else evac_copy)(kTb[zp], 1.0), zdeps)
                if hi + 1 < len(heads):
                    loads = issue_loads(hi + 1)
                if kind == "mm":
                    proj_tm(wvv, tv, lambda c, tt: memnT[:, c, tt * 128:(tt + 1) * 128], 2, vb[zp], zdeps)
                else:
                    proj_tm(wvv, tv, lambda c, tt: hT[:, c, tt * 128:(tt + 1) * 128], 16, vb[zp], zdeps)
                proj_fm(wg, tg, hT_rhs, 512, evac_silu(sgb[zp]), zdeps)
                PE.wait([ACT.last, DVE.last] + y_res[yp].readers)
                DVE.wait(y_res[yp].readers)
                if kind == "na":
                    attn_na(zp, yp, cur["bt"][0], cur["bt"][1])
                elif kind == "dl":
                    items = []
                    for b in range(4):
                        js = list(range(max(0, 4 * b - 8), min(15, 4 * b + 11) + 1))
                        for j in js:
                            items.append((b, kTb[zp][:, j * 128:(j + 1) * 128], vb[zp][:, j * 128:(j + 1) * 128],
                                          W_OFF - 128 * j + 512 * b, j == js[0], j == js[-1]))
                    attn_blocks(zp, yp, items, True)
                else:
                    items = []
                    for b in range(4):
                        for mt_ in range(2):
                            items.append((b, kTb[zp][:, mt_ * 128:(mt_ + 1) * 128],
                                          vb[zp][:, mt_ * 128:(mt_ + 1) * 128], None, mt_ == 0, mt_ == 1))
                    attn_blocks(zp, yp, items, False)
                gh = {"na": 0, "dl": 6, "mm": 12}[kind] + h
                store_y(yp, par, gh)

        def phase_C(s, l, par, src, dst):
            ytoks = []
            for gh in range(16):
                ytoks.append(dma(SP, ysem, hT[:, gh, :], ybuf[par, gh * 128:(gh + 1) * 128, :]))
            for cb in range(4):
                wl = [load_wblock(w_out, l, cb * 512 + i * 128) for i in range(4)]
                s0 = wl[0][0]
                assert s0 % 4 == 0
                wtoks = [t for (_, t) in wl]
                mt = None
                for tt in range(16):
                    bank, bres = next_pj()
                    k = cnt["xt"]
                    cnt["xt"] += 1
                    sl = k % 2
                    r0 = s * T + tt * 128
                    tl = dma(SP, xld[sl], xt[sl][:, 0:512], src[r0:r0 + 128, cb * 512:(cb + 1) * 512],
                             deps=xt_res[sl].readers)
                    PE.wait(wtoks + ytoks + bres.readers)
                    for i in range(4):
                        wv = wslot_view(s0 + i)
                        for gh in range(16):
                            ins = nc.tensor.matmul(bank[:, i * 128:(i + 1) * 128], hT[:, gh, tt * 128:(tt + 1) * 128],
                                                   wv[:, gh, :], start=(gh == 0), stop=(gh == 15))
                    mt = PE.op(ins)
                    DVE.wait([mt, tl])
                    d1 = DVE.op(nc.vector.tensor_tensor(out=xt[sl][:, 0:512], in0=bank[:, :], in1=xt[sl][:, 0:512],
                                                        op=ALU.add))
                    bres.readers = [d1]
                    st_ = dma(SP, xst[sl], dst[r0:r0 + 128, cb * 512:(cb + 1) * 512], xt[sl][:, 0:512], deps=[d1])
                    xt_res[sl].readers = [st_]
                for i in range(4):
                    w_res[s0 + i].readers = [mt]

        def phase_D(s, src):
            dma(SP, cs_sem, fgbc, fg_in[:, :])
            barrier()
            for tt in range(16):
                k = cnt["xt"]
                cnt["xt"] += 1
                sl = k % 2
                r0 = s * T + tt * 128
                tl = dma(SP, xld[sl], xt[sl], src[r0:r0 + 128, :], deps=xt_res[sl].readers)
                a1, a3, rstd = rstd_chain(xt[sl], xs[sl], [tl] + xs_res[sl].readers)
                DVE.wait([a3])
                d1 = DVE.op(nc.vector.scalar_tensor_tensor(out=xt[sl], in0=xt[sl], scalar=rstd, in1=fgbc,
                                                           op0=ALU.mult, op1=ALU.mult))
                st_ = dma(SP, xst[sl], y_out[r0:r0 + 128, :], xt[sl], deps=[d1])
                xt_res[sl].readers = [st_]
                xs_res[sl].readers = [a1]

        ysem = DSem(ctx, "yld")
        ucount = 0
        for s in range(NSEQ):
            mem_norm(s)
            barrier()
            for li, l in enumerate(layers):
                par = ucount % 2
                ucount += 1
                src = x_in if li == 0 else xbuf
                is_last_layer = (li == len(layers) - 1)
                dst = xbuf if (not is_last_layer or last) else y_out
                phase_norm(src, s * T, 16, l, hT)
                barrier()
                phase_B(s, l, par)
                barrier()
                phase_C(s, l, par, src, dst)
                barrier()
            if last:
                phase_D(s, xbuf)
                barrier()
        barrier()
    return nc


def kernel(x, mem, norm_g, w_in, na_rpb, mem_norm_g, w_mem_kv, w_out, final_g):
    x = np.asarray(x, np.float32)
    mem = np.asarray(mem, np.float32)
    nabt, plan, nt = na_bias_tables(np.asarray(na_rpb, np.float32))
    cst32, cst16 = const_tables()
    gains = np.concatenate([np.asarray(norm_g, np.float32), np.asarray(final_g, np.float32)[None],
                            np.asarray(mem_norm_g, np.float32)[None]], 0)
    gTh = np.ascontiguousarray(gains.reshape(6, NCH, 128).transpose(2, 0, 1)).reshape(128, 96)
    fgbc = np.ascontiguousarray(np.broadcast_to(np.asarray(final_g, np.float32)[None, :], (128, D)))
    w_in2 = np.ascontiguousarray(np.asarray(w_in, np.float32).reshape(DEPTH * D, IN_COLS))
    w_mem2 = np.ascontiguousarray(np.asarray(w_mem_kv, np.float32).reshape(DEPTH * D, 1024))
    w_out2 = np.ascontiguousarray(np.asarray(w_out, np.float32).reshape(DEPTH * D, D))

    cur = [np.ascontiguousarray(x[2 * c:2 * c + 2].reshape(NSEQ * T, D)) for c in range(N_CORES)]
    mems = [np.ascontiguousarray(mem[2 * c:2 * c + 2].reshape(NSEQ * N_MEM, D)) for c in range(N_CORES)]
    groups = [list(range(DEPTH))[i:i + LAYERS_PER_LAUNCH] for i in range(0, DEPTH, LAYERS_PER_LAUNCH)]
    for gi, layers in enumerate(groups):
        nc = build_program(layers, gi == 0, gi == len(groups) - 1, nt * 128, plan)
        in_maps = [{"x": cur[c], "mem": mems[c], "gT": gTh, "fgbc": fgbc, "w_in": w_in2, "w_mem": w_mem2,
                    "w_out": w_out2, "nab": nabt, "cst32": cst32, "cst16": cst16} for c in range(N_CORES)]
        res = run_bass_kernel_spmd(nc, in_maps, core_ids=list(range(N_CORES)))
        cur = [np.asarray(res.results[c]["y"]) for c in range(N_CORES)]
    out = np.stack([cur[c].reshape(NSEQ, T, D) for c in range(N_CORES)], 0).reshape(2 * N_CORES, T, D)
    return out.astype(np.float32, copy=False)
```

```python
import contextlib
import numpy as np
import concourse.bass as bass
import concourse.mybir as mybir
from concourse.bass_utils import run_bass_kernel_spmd

F32 = mybir.dt.float32
BF16 = mybir.dt.bfloat16
AF = mybir.ActivationFunctionType
ALU = mybir.AluOpType

N_CORES = 8
D = 2048
T = 2048
NSEQ = 2
DEPTH = 4
HD = 128
NCH = 16
IN_COLS = 7168
N_MEM = 256
SCALE = float(HD ** -0.5)
EPS = 1e-6
NEG = -30000.0
SEM_LIMIT = 60000
W_OFF = 1408
W_LEN = 2944

LAYERS_PER_LAUNCH = 4


def _rs(r):
    return min(max(r - 4, 0), 24)


def _cs(c):
    return min(max(c - 8, 0), 48)


def na_plan():
    types = {}
    type_list = []
    plan = []
    for qi in range(16):
        lst = []
        for j in range(16):
            valid = []
            for krl in range(2):
                for qrl in range(2):
                    kr = 2 * j + krl
                    qr = 2 * qi + qrl
                    valid.append(_rs(qr) <= kr < _rs(qr) + 8)
            if not any(valid):
                continue
            key = (j - qi, tuple(valid))
            if key not in types:
                types[key] = len(type_list)
                type_list.append((qi, j))
            lst.append((j, types[key]))
        plan.append(lst)
    return plan, type_list


def na_bias_tables(na_rpb):
    plan, type_list = na_plan()
    nt = len(type_list)
    dr_idx = np.zeros((nt, 128, 128), np.int64)
    dc_idx = np.zeros((nt, 128, 128), np.int64)
    ok = np.zeros((nt, 128, 128), bool)
    p = np.arange(128)
    krl, kc = p // 64, p % 64
    qrl, qc = p // 64, p % 64
    cs = np.clip(qc - 8, 0, 48)
    for t, (qi, j) in enumerate(type_list):
        kr = 2 * j + krl[:, None]
        qr = 2 * qi + qrl[None, :]
        rs = np.clip(qr - 4, 0, 24)
        row_ok = (kr >= rs) & (kr < rs + 8)
        col_ok = (kc[:, None] >= cs[None, :]) & (kc[:, None] < cs[None, :] + 16)
        ok[t] = row_ok & col_ok
        dr_idx[t] = np.clip(kr - qr + 7, 0, 14)
        dc_idx[t] = np.clip(kc[:, None] - qc[None, :], -15, 15) + 15
    g = na_rpb[:, :, dr_idx, dc_idx]
    g = np.where(ok[None, None], g, np.float32(NEG)).astype(np.float32)
    g = np.ascontiguousarray(g.transpose(0, 1, 3, 2, 4)).reshape(DEPTH * 6 * 128, nt * 128)
    return g, plan, nt


def const_tables():
    half = HD // 2
    inv = (np.float32(10000.0) ** (-(np.arange(half, dtype=np.float32) / np.float32(half)))).astype(np.float32)
    ang = (np.arange(T, dtype=np.float32)[:, None] * inv[None, :]).astype(np.float32)
    cos = np.cos(ang.astype(np.float64)).astype(np.float32).T
    sin = np.sin(ang.astype(np.float64)).astype(np.float32).T
    cosT = np.concatenate([cos, cos], 0)
    sinS = np.concatenate([-sin, sin], 0)
    swap = np.zeros((128, 128), np.float32)
    for m in range(128):
        swap[(m + 64) % 128, m] = 1.0
    ident = np.eye(128, dtype=np.float32)
    ones = np.ones((128, 128), np.float32)
    p = np.arange(128)[:, None]
    u = np.arange(W_LEN)[None, :]
    d = p - u + W_OFF
    ad = np.abs(d)
    w = (ad <= 64).astype(np.float32) + ((d % 4 == 0) & (ad <= 256)).astype(np.float32) \
        + ((d % 16 == 0) & (ad <= 1024)).astype(np.float32)
    cst32 = np.concatenate([cosT, sinS, swap], 1).astype(np.float32)
    cst16 = np.concatenate([ident, ones, w], 1).astype(np.float32)
    return cst32, cst16


def _flat(deps):
    out = []
    for d in deps:
        if d is None:
            continue
        if isinstance(d, list):
            out.extend(_flat(d))
        else:
            out.append(d)
    return out


class Ctx:
    def __init__(self, nc, stack):
        self.nc = nc
        self.stack = stack
        self.nsem = 0

    def new_sem(self, name):
        self.nsem += 1
        return self.stack.enter_context(self.nc.semaphore(f"{name}_{self.nsem}"))


class Eng:
    def __init__(self, ctx, e, name):
        self.ctx = ctx
        self.e = e
        self.name = name
        self.sem = ctx.new_sem(name)
        self.n = 0
        self.seen = {}
        self.last = None

    def wait(self, deps):
        for (sem, val) in _flat(deps):
            k = id(sem)
            if self.seen.get(k, 0) >= val:
                continue
            self.e.wait_ge(sem, val)
            self.seen[k] = val

    def op(self, ins, sig=True):
        if not sig:
            return None
        if self.n >= SEM_LIMIT:
            self.sem = self.ctx.new_sem(self.name)
            self.n = 0
        self.n += 1
        ins.then_inc(self.sem, 1)
        self.last = (self.sem, self.n)
        return self.last


class DSem:
    def __init__(self, ctx, name):
        self.ctx = ctx
        self.name = name
        self.sem = ctx.new_sem(name)
        self.n = 0

    def inc(self, ins):
        if self.n >= SEM_LIMIT:
            self.sem = self.ctx.new_sem(self.name)
            self.n = 0
        self.n += 16
        ins.then_inc(self.sem, 16)
        return (self.sem, self.n)


def build_program(layers, first, last, nab_cols, plan):
    nc = bass.Bass("TRN2", target_bir_lowering=False)
    NT128 = nab_cols
    x_in = nc.dram_tensor("x", [NSEQ * T, D], F32, kind="ExternalInput").ap()
    mem_in = nc.dram_tensor("mem", [NSEQ * N_MEM, D], F32, kind="ExternalInput").ap()
    gT_in = nc.dram_tensor("gT", [128, 96], F32, kind="ExternalInput").ap()
    fg_in = nc.dram_tensor("fgbc", [128, D], F32, kind="ExternalInput").ap()
    w_in = nc.dram_tensor("w_in", [DEPTH * D, IN_COLS], F32, kind="ExternalInput").ap()
    w_mem = nc.dram_tensor("w_mem", [DEPTH * D, 1024], F32, kind="ExternalInput").ap()
    w_out = nc.dram_tensor("w_out", [DEPTH * D, D], F32, kind="ExternalInput").ap()
    nab = nc.dram_tensor("nab", [DEPTH * 6 * 128, NT128], F32, kind="ExternalInput").ap()
    c32_in = nc.dram_tensor("cst32", [128, 2 * T + 128], F32, kind="ExternalInput").ap()
    c16_in = nc.dram_tensor("cst16", [128, 256 + W_LEN], F32, kind="ExternalInput").ap()
    y_out = nc.dram_tensor("y", [NSEQ * T, D], F32, kind="ExternalOutput").ap()
    xbuf = nc.dram_tensor("xbuf", [NSEQ * T, D], F32).ap()
    ybuf = nc.dram_tensor("ybuf", [2, D, T], BF16).ap()

    sb = nc.alloc_sbuf_tensor
    hT = sb("hT", [128, NCH, T], BF16)
    memnT = sb("memnT", [128, NCH, N_MEM], BF16)
    wbuf = sb("wbuf", [128, 8 * NCH * 128], BF16)
    cosT = sb("cosT", [128, T], F32)
    sinS = sb("sinS", [128, T], F32)
    swapm = sb("swapm", [128, 128], F32)
    c16 = sb("c16", [128, 256 + W_LEN], BF16)
    ident = c16[:, 0:128]
    ones = c16[:, 128:256]
    gT = sb("gTs", [128, 96], F32)
    stats = sb("stats", [128, 640], F32)
    U = sb("U", [128, 10 * T], BF16)
    qTb = [U[:, (0 + i) * T:(1 + i) * T] for i in range(2)]
    kTb = [U[:, (2 + i) * T:(3 + i) * T] for i in range(2)]
    sgb = [U[:, (4 + i) * T:(5 + i) * T] for i in range(2)]
    vb = [U[:, (6 + i) * T:(7 + i) * T] for i in range(2)]
    yTb = [U[:, (8 + i) * T:(9 + i) * T] for i in range(2)]
    xt = [U[:, (2 * i) * T:(2 * i + 2) * T].bitcast(F32) for i in range(2)]
    xs = [U[:, (4 + i) * T:(5 + i) * T] for i in range(2)]
    fgbc = U[:, 6 * T:8 * T].bitcast(F32)
    xc = [U[:, i * 1024:(i + 1) * 1024].bitcast(F32) for i in range(8)]
    PTb = [sb(f"PT{i}", [128, 640], BF16) for i in range(4)]
    Eb = [sb(f"E{i}", [128, 512], BF16) for i in range(3)]
    Btb = [sb(f"Bt{i}", [128, NT128], BF16) for i in range(2)]
    f32tmp = [sb(f"ft{i}", [128, 512], F32) for i in range(8)]
    qfb = [sb(f"qf{i}", [128, 512], F32) for i in range(3)]

    pjb = [nc.alloc_psum_tensor(f"pj{i}", [128, 512], F32) for i in range(2)]
    sbig = [nc.alloc_psum_tensor(f"sS{i}", [128, 1024], F32) for i in range(2)]
    odb = [nc.alloc_psum_tensor(f"od{i}", [128, 512], F32) for i in range(2)]

    with contextlib.ExitStack() as stack:
        ctx = Ctx(nc, stack)
        PE = Eng(ctx, nc.tensor, "pe")
        ACT = Eng(ctx, nc.scalar, "act")
        DVE = Eng(ctx, nc.vector, "dve")
        POOL = Eng(ctx, nc.gpsimd, "pool")
        SP = Eng(ctx, nc.sync, "sp")
        engines = [PE, ACT, DVE, POOL, SP]
        pending_dma = []

        def dma(eng, dsem, out, in_, deps=()):
            eng.wait(deps)
            tok = dsem.inc(eng.e.dma_start(out=out, in_=in_))
            pending_dma.append(tok)
            return tok

        def barrier():
            toks = [e.last for e in engines if e.last is not None] + list(pending_dma)
            for e in engines:
                e.wait(toks)
            del pending_dma[:]

        class Res:
            def __init__(self):
                self.readers = []
                self.writer = None

        cs_sem = DSem(ctx, "cst")
        t_c = [dma(SP, cs_sem, cosT[:], c32_in[:, 0:T]),
               dma(SP, cs_sem, sinS[:], c32_in[:, T:2 * T]),
               dma(SP, cs_sem, swapm[:], c32_in[:, 2 * T:2 * T + 128]),
               dma(SP, cs_sem, gT[:], gT_in[:, :]),
               dma(POOL, cs_sem, c16[:], c16_in[:, :])]
        t_c.append(DVE.op(nc.vector.memset(stats[:], 0.0)))
        barrier()

        stat_col = [0]
        xt_res = [Res(), Res()]
        xs_res = [Res(), Res()]
        pj_res = [Res(), Res()]
        xld = [DSem(ctx, "xld0"), DSem(ctx, "xld1")]
        xst = [DSem(ctx, "xst0"), DSem(ctx, "xst1")]
        xc_res = [Res() for _ in range(8)]
        xcld = [DSem(ctx, f"xcld{i}") for i in range(8)]
        xcst = [DSem(ctx, f"xcst{i}") for i in range(8)]
        cnt = {"qf": 0, "xc": 0, "xt": 0, "pj": 0, "w": 0, "z": 0, "y": 0, "S": 0, "S4": 0, "PT": 0, "E": 0, "od": 0, "ft": 0, "bt": 0}

        def rstd_chain(xtile, junk, deps):
            c = stat_col[0]
            stat_col[0] += 3
            assert stat_col[0] <= 640
            ACT.wait(deps)
            a1 = ACT.op(nc.scalar.activation(out=junk, in_=xtile, func=AF.Square, accum_out=stats[:, c:c + 1]))
            ACT.wait([a1])
            a2 = ACT.op(nc.scalar.activation(out=stats[:, c + 1:c + 2], in_=stats[:, c:c + 1], func=AF.Ln,
                                             scale=1.0 / D, bias=EPS))
            ACT.wait([a2])
            a3 = ACT.op(nc.scalar.activation(out=stats[:, c + 2:c + 3], in_=stats[:, c + 1:c + 2], func=AF.Exp,
                                             scale=-0.5))
            return a1, a3, stats[:, c + 2:c + 3]

        def phase_norm(src, r0, ntiles, gidx, dst):
            for tt in range(ntiles):
                k = cnt["xt"]
                cnt["xt"] += 1
                sl = k % 2
                tl = dma(SP, xld[sl], xt[sl], src[r0 + tt * 128:r0 + (tt + 1) * 128, :], deps=xt_res[sl].readers)
                a1, a3, rstd = rstd_chain(xt[sl], xs[sl], [tl] + xs_res[sl].readers)
                DVE.wait([a3])
                d1 = DVE.op(nc.vector.tensor_scalar(out=xs[sl], in0=xt[sl], scalar1=rstd, scalar2=None,
                                                    op0=ALU.mult))
                xt_res[sl].readers = [a1, d1]
                tps = []
                for half in range(2):
                    kb = cnt["pj"]
                    cnt["pj"] += 1
                    bank = pjb[kb % 2]
                    bres = pj_res[kb % 2]
                    pb = bank[:].bitcast(BF16)
                    PE.wait([d1] + bres.readers)
                    for c8 in range(8):
                        c = half * 8 + c8
                        ins = nc.tensor.transpose(out=pb[:, c8 * 128:(c8 + 1) * 128],
                                                  in_=xs[sl][:, c * 128:(c + 1) * 128], identity=ident)
                    tp = PE.op(ins)
                    tps.append(tp)
                    DVE.wait([tp])
                    gb = gT[:, gidx * 16 + half * 8:gidx * 16 + half * 8 + 8].unsqueeze(2).to_broadcast([128, 8, 128])
                    e1 = DVE.op(nc.vector.tensor_tensor(
                        out=dst[:, half * 8:(half + 1) * 8, tt * 128:(tt + 1) * 128],
                        in0=pb[:, 0:1024].rearrange("p (c t) -> p c t", c=8), in1=gb, op=ALU.mult))
                    bres.readers = [e1]
                xs_res[sl].readers = tps

        w_res = [Res() for _ in range(8)]
        w_sem = [DSem(ctx, f"w{i}") for i in range(8)]

        def wslot_view(s):
            return wbuf[:, s * 2048:(s + 1) * 2048].rearrange("p (c n) -> p c n", c=NCH)

        def load_wblock(wt, layer, c0):
            s = cnt["w"] % 8
            cnt["w"] += 1
            src = wt[layer * D:(layer + 1) * D, c0:c0 + 128].rearrange("(c p) n -> p c n", p=128)
            tok = dma(POOL, w_sem[s], wslot_view(s), src, deps=w_res[s].readers)
            w_res[s].writer = tok
            return s, tok

        z_res = [Res(), Res()]
        y_res = [Res(), Res()]
        yst = [DSem(ctx, "yst0"), DSem(ctx, "yst1")]
        S_res = [Res(), Res()]
        S4_res = [Res() for _ in range(4)]
        PT_res = [Res() for _ in range(4)]
        E_res = [Res() for _ in range(3)]
        od_res = [Res(), Res()]
        ft_res = [Res() for _ in range(8)]
        bt_res = [Res(), Res()]
        bt_sem = [DSem(ctx, "bt0"), DSem(ctx, "bt1")]

        def S4_view(i):
            return sbig[i // 2][:, (i % 2) * 512:(i % 2 + 1) * 512]

        def next_ft():
            k = cnt["ft"] % 8
            cnt["ft"] += 1
            return f32tmp[k], ft_res[k]

        def next_pj():
            k = cnt["pj"] % 2
            cnt["pj"] += 1
            return pjb[k], pj_res[k]

        def proj_fm(ws, wtok, rhs_of, ncols, evac, zdeps):
            wv = wslot_view(ws)
            nb_ = (T if ncols == 512 else N_MEM) // ncols
            mt = None
            for tb in range(nb_):
                bank, bres = next_pj()
                PE.wait([wtok] + bres.readers + zdeps)
                for c in range(NCH):
                    ins = nc.tensor.matmul(bank[:, 0:ncols], wv[:, c, :], rhs_of(c, tb), start=(c == 0),
                                           stop=(c == NCH - 1))
                mt = PE.op(ins)
                bres.readers = _flat([evac(tb, bank, mt)])
                w_res[ws].readers = [mt]
                yield

        def proj_tm(ws, wtok, lhs_of, ntiles, dst, zdeps):
            wv = wslot_view(ws)
            mt = None
            ngrp = (ntiles + 3) // 4
            for g in range(ngrp):
                bank, bres = next_pj()
                nt_ = min(4, ntiles - g * 4)
                PE.wait([wtok] + bres.readers + zdeps)
                for t4 in range(nt_):
                    tt = g * 4 + t4
                    for c in range(NCH):
                        ins = nc.tensor.matmul(bank[:, t4 * 128:(t4 + 1) * 128], lhs_of(c, tt), wv[:, c, :],
                                               start=(c == 0), stop=(c == NCH - 1))
                mt = PE.op(ins)
                ACT.wait([mt])
                ev = ACT.op(nc.scalar.copy(out=dst[:, g * 512:g * 512 + nt_ * 128], in_=bank[:, 0:nt_ * 128]))
                bres.readers = [ev]
                w_res[ws].readers = [mt]
                yield

        def hT_rhs(c, tb):
            return hT[:, c, tb * 512:(tb + 1) * 512]

        def evac_copy(dst, scale):
            def f(tb, bank, tok):
                ACT.wait([tok])
                n = 512
                return ACT.op(nc.scalar.activation(out=dst[:, tb * n:(tb + 1) * n], in_=bank[:, 0:n], func=AF.Copy,
                                                   scale=scale))
            return f

        rope_pending = []
        qf_res = [Res() for _ in range(3)]

        def flush_rope():
            while rope_pending:
                rope_pending.pop(0)()

        def evac_rope(dst, scale):
            def f(tb, bank, tok):
                flush_rope()
                kq = cnt["qf"] % 3
                cnt["qf"] += 1
                qf, qres = qfb[kq], qf_res[kq]
                ACT.wait([tok] + qres.readers)
                a = ACT.op(nc.scalar.activation(out=qf[:], in_=bank[:, 0:512], func=AF.Copy, scale=scale))
                qres.readers = [a]

                def rest():
                    k = cnt["S4"] % 4
                    cnt["S4"] += 1
                    rp, rres = S4_view(k), S4_res[k]
                    PE.wait([a] + rres.readers)
                    r = PE.op(nc.tensor.matmul(rp, swapm[:], qf[:], start=True, stop=True))
                    t1, t1res = next_ft()
                    DVE.wait([a] + t1res.readers)
                    d1 = DVE.op(nc.vector.tensor_tensor(out=t1[:], in0=qf[:], in1=cosT[:, tb * 512:(tb + 1) * 512],
                                                        op=ALU.mult))
                    t2, t2res = next_ft()
                    DVE.wait([r] + t2res.readers)
                    d2 = DVE.op(nc.vector.tensor_tensor(out=t2[:], in0=rp, in1=sinS[:, tb * 512:(tb + 1) * 512],
                                                        op=ALU.mult))
                    rres.readers = [d2]
                    DVE.wait([d1, d2])
                    d3 = DVE.op(nc.vector.tensor_tensor(out=dst[:, tb * 512:(tb + 1) * 512], in0=t1[:], in1=t2[:],
                                                        op=ALU.add))
                    qres.readers = [r, d1]
                    t1res.readers = [d3]
                    t2res.readers = [d3]
                rope_pending.append(rest)
                return a
            return f

        def evac_silu(dst):
            def f(tb, bank, tok):
                te, teres = next_ft()
                t2, t2res = next_ft()
                ACT.wait([tok] + teres.readers + t2res.readers)
                a1 = ACT.op(nc.scalar.activation(out=te[:], in_=bank[:, 0:512], func=AF.Exp, scale=-1.0))
                ACT.wait([a1])
                a2 = ACT.op(nc.scalar.activation(out=t2[:], in_=te[:], func=AF.Ln, bias=1.0))
                ACT.wait([a2])
                a3 = ACT.op(nc.scalar.activation(out=te[:], in_=t2[:], func=AF.Exp, scale=-1.0))
                DVE.wait([a3])
                d3 = DVE.op(nc.vector.tensor_tensor(out=dst[:, tb * 512:(tb + 1) * 512], in0=bank[:, 0:512],
                                                    in1=te[:], op=ALU.mult))
                teres.readers = [d3]
                t2res.readers = [a3]
                return [d3]
            return f

        def normalize(ptok, o_ap, d_ap, n, ydst, sg_ap):
            rd, rres = next_ft()
            tq, tres = next_ft()
            DVE.wait([ptok] + rres.readers)
            d1 = DVE.op(nc.vector.reciprocal(out=rd[:, 0:n], in_=d_ap))
            DVE.wait([d1] + tres.readers)
            d2 = DVE.op(nc.vector.tensor_tensor(out=tq[:, 0:n], in0=o_ap, in1=rd[:, 0:n], op=ALU.mult))
            DVE.wait([d2])
            d3 = DVE.op(nc.vector.tensor_tensor(out=ydst, in0=tq[:, 0:n], in1=sg_ap, op=ALU.mult))
            rres.readers = [d2]
            tres.readers = [d3]
            return d3, d2, d1

        def store_y(yp, par, gh):
            tok = dma(SP, yst[yp], ybuf[par, gh * 128:(gh + 1) * 128, :], yTb[yp], deps=[DVE.last])
            y_res[yp].readers = [tok]

        def attn_na(zp, yp, btp, bt_tok):
            qT, kT, sg, v, yT = qTb[zp], kTb[zp], sgb[zp], vb[zp], yTb[yp]
            Bt = Btb[btp]
            prev = None
            last_pe = None

            def pv(st):
                qi, J, pk, et = st
                ko = cnt["od"] % 2
                cnt["od"] += 1
                od, ores = odb[ko], od_res[ko]
                PE.wait([et] + ores.readers)
                nj = len(J)
                for jj, (j, ty) in enumerate(J):
                    nc.tensor.matmul(od[:, 0:128], v[:, j * 128:(j + 1) * 128], PTb[pk][:, jj * 128:(jj + 1) * 128],
                                     start=(jj == 0), stop=(jj == nj - 1))
                for jj, (j, ty) in enumerate(J):
                    ins = nc.tensor.matmul(od[:, 128:256], ones, PTb[pk][:, jj * 128:(jj + 1) * 128],
                                           start=(jj == 0), stop=(jj == nj - 1))
                pt = PE.op(ins)
                PT_res[pk].readers = [pt]
                d3, c1, a1 = normalize(pt, od[:, 0:128], od[:, 128:256], 128, yT[:, qi * 128:(qi + 1) * 128],
                                       sg[:, qi * 128:(qi + 1) * 128])
                ores.readers = [c1, a1]
                return pt

            for qi in range(16):
                J = plan[qi]
                ks = cnt["S"] % 2
                cnt["S"] += 1
                sbk, sres = sbig[ks], S_res[ks]
                PE.wait(sres.readers + S4_res[2 * ks].readers + S4_res[2 * ks + 1].readers + [bt_tok])
                for jj, (j, ty) in enumerate(J):
                    nc.tensor.matmul(sbk[:, jj * 128:(jj + 1) * 128], kT[:, j * 128:(j + 1) * 128],
                                     qT[:, qi * 128:(qi + 1) * 128], start=True, stop=False)
                    ins = nc.tensor.matmul(sbk[:, jj * 128:(jj + 1) * 128], ident, Bt[:, ty * 128:(ty + 1) * 128],
                                           start=False, stop=True)
                st = PE.op(ins)
                pk = cnt["PT"] % 4
                cnt["PT"] += 1
                n = len(J) * 128
                ACT.wait([st] + PT_res[pk].readers)
                n1 = min(n, 512)
                et = ACT.op(nc.scalar.activation(out=PTb[pk][:, 0:n1], in_=sbk[:, 0:n1], func=AF.Exp))
                if n > 512:
                    et = ACT.op(nc.scalar.activation(out=PTb[pk][:, 512:n], in_=sbk[:, 512:n], func=AF.Exp))
                sres.readers = [et]
                S4_res[2 * ks].readers = [et]
                S4_res[2 * ks + 1].readers = [et]
                if prev is not None:
                    last_pe = pv(prev)
                prev = (qi, J, pk, et)
                yield
            last_pe = pv(prev)
            z_res[zp].readers = [last_pe, DVE.last]
            bt_res[btp].readers = [last_pe]

        def attn_blocks(zp, yp, items, masked):
            qT, sg, yT = qTb[zp], sgb[zp], yTb[yp]
            LA = 3
            staged = []
            last_pe = [None]

            def pv(st):
                b, vap, pk, mtok, first, lastf = st
                if first:
                    PE.wait(od_res[0].readers + od_res[1].readers)
                PE.wait([mtok])
                nc.tensor.matmul(odb[0][:, :], vap, PTb[pk][:, 0:512], start=first, stop=lastf)
                pt = PE.op(nc.tensor.matmul(odb[1][:, :], ones, PTb[pk][:, 0:512], start=first, stop=lastf))
                PT_res[pk].readers = [pt]
                last_pe[0] = pt
                if lastf:
                    d3, c1, a1 = normalize(pt, odb[0][:, :], odb[1][:, :], 512, yT[:, b * 512:(b + 1) * 512],
                                           sg[:, b * 512:(b + 1) * 512])
                    od_res[0].readers = [c1]
                    od_res[1].readers = [a1]

            for (b, kap, vap, moff, first, lastf) in items:
                k4 = cnt["S4"] % 4
                cnt["S4"] += 1
                sv_, sres = S4_view(k4), S4_res[k4]
                PE.wait(sres.readers)
                st = PE.op(nc.tensor.matmul(sv_, kap, qT[:, b * 512:(b + 1) * 512], start=True, stop=True))
                pk = cnt["PT"] % 4
                cnt["PT"] += 1
                if masked:
                    ke = cnt["E"] % 3
                    cnt["E"] += 1
                    ACT.wait([st] + E_res[ke].readers)
                    et = ACT.op(nc.scalar.activation(out=Eb[ke][:], in_=sv_, func=AF.Exp))
                    sres.readers = [et]
                    DVE.wait([et] + PT_res[pk].readers)
                    mtok = DVE.op(nc.vector.tensor_tensor(out=PTb[pk][:, 0:512], in0=Eb[ke][:],
                                                          in1=c16[:, 256 + moff:256 + moff + 512], op=ALU.mult))
                    E_res[ke].readers = [mtok]
                else:
                    ACT.wait([st] + PT_res[pk].readers)
                    mtok = ACT.op(nc.scalar.activation(out=PTb[pk][:, 0:512], in_=sv_, func=AF.Exp))
                    sres.readers = [mtok]
                staged.append((b, vap, pk, mtok, first, lastf))
                if len(staged) > LA:
                    pv(staged.pop(0))
                yield
            while staged:
                pv(staged.pop(0))
            z_res[zp].readers = [last_pe[0], DVE.last]

        def mem_norm(s):
            phase_norm(mem_in, s * N_MEM, 2, 5, memnT)

        def phase_B(s, l, par):
            heads = [("na", h) for h in range(6)] + [("dl", h) for h in range(6)] + [("mm", h) for h in range(4)]
            def wblocks(kind, h):
                if kind == "na":
                    return [(w_in, 0 + h * 128), (w_in, 768 + h * 128), (w_in, 1536 + h * 128), (w_in, 2304 + h * 128)]
                if kind == "dl":
                    return [(w_in, 3072 + h * 128), (w_in, 3840 + h * 128), (w_in, 4608 + h * 128),
                            (w_in, 5376 + h * 128)]
                return [(w_in, 6144 + h * 128), (w_mem, h * 128), (w_mem, 512 + h * 128), (w_in, 6656 + h * 128)]

            def issue_loads(idx):
                kind, h = heads[idx]
                res = {}
                if kind == "na":
                    kb = cnt["bt"] % 2
                    cnt["bt"] += 1
                    r0 = (l * 6 + h) * 128
                    res["bt"] = (kb, dma(POOL, bt_sem[kb], Btb[kb][:], nab[r0:r0 + 128, :], deps=bt_res[kb].readers))
                res["w"] = [load_wblock(wt, l, c0) for (wt, c0) in wblocks(kind, h)]
                return res

            nH = len(heads)
            loads = {}

            def issue_w(idx):
                kind, h = heads[idx]
                loads.setdefault(idx, {})["w"] = [load_wblock(wt, l, c0) for (wt, c0) in wblocks(kind, h)]

            def issue_bt(idx):
                kind, h = heads[idx]
                if kind != "na":
                    return
                kb = cnt["bt"] % 2
                cnt["bt"] += 1
                r0 = (l * 6 + h) * 128
                loads.setdefault(idx, {})["bt"] = (kb, dma(POOL, bt_sem[kb], Btb[kb][:], nab[r0:r0 + 128, :],
                                                          deps=bt_res[kb].readers))

            def P_gen(hi):
                kind, h = heads[hi]
                cur = loads[hi]
                zp = hi % 2
                zdeps = list(z_res[zp].readers)
                (wq, tq), (wk, tk), (wvv, tv), (wg, tg) = cur["w"]
                rope = kind == "dl"
                yield from proj_fm(wq, tq, hT_rhs, 512, (evac_rope if rope else evac_copy)(qTb[zp], SCALE), zdeps)
                if kind == "mm":
                    def ev_mk(tb, bank, tok):
                        ACT.wait([tok])
                        return ACT.op(nc.scalar.copy(out=kTb[zp][:, 0:N_MEM], in_=bank[:, 0:N_MEM]))
                    yield from proj_fm(wk, tk, lambda c, tb: memnT[:, c, :], N_MEM, ev_mk, zdeps)
                    yield from proj_tm(wvv, tv, lambda c, tt: memnT[:, c, tt * 128:(tt + 1) * 128], 2, vb[zp], zdeps)
                else:
                    yield from proj_fm(wk, tk, hT_rhs, 512, (evac_rope if rope else evac_copy)(kTb[zp], 1.0), zdeps)
                    yield from proj_tm(wvv, tv, lambda c, tt: hT[:, c, tt * 128:(tt + 1) * 128], 16, vb[zp], zdeps)
                flush_rope()
                yield from proj_fm(wg, tg, hT_rhs, 512, evac_silu(sgb[zp]), zdeps)

            def A_gen(hi):
                kind, h = heads[hi]
                cur = loads[hi]
                zp = hi % 2
                yp = hi % 2
                PE.wait([ACT.last, DVE.last] + y_res[yp].readers)
                DVE.wait(y_res[yp].readers)
                if kind == "na":
                    yield from attn_na(zp, yp, cur["bt"][0], cur["bt"][1])
                elif kind == "dl":
                    items = []
                    for b in range(4):
                        js = list(range(max(0, 4 * b - 8), min(15, 4 * b + 11) + 1))
                        for j in js:
                            items.append((b, kTb[zp][:, j * 128:(j + 1) * 128], vb[zp][:, j * 128:(j + 1) * 128],
                                          W_OFF - 128 * j + 512 * b, j == js[0], j == js[-1]))
                    yield from attn_blocks(zp, yp, items, True)
                else:
                    items = []
                    for b in range(4):
                        for mt_ in range(2):
                            items.append((b, kTb[zp][:, mt_ * 128:(mt_ + 1) * 128],
                                          vb[zp][:, mt_ * 128:(mt_ + 1) * 128], None, mt_ == 0, mt_ == 1))
                    yield from attn_blocks(zp, yp, items, False)
                gh = {"na": 0, "dl": 6, "mm": 12}[kind] + h
                store_y(yp, par, gh)

            def interleave(ga, gb, na, nb):
                ratio = (nb / na) if gb is not None else 0.0
                acc = 0.0
                b_done = gb is None
                for _ in ga:
                    acc += ratio
                    while acc >= 1.0 and not b_done:
                        acc -= 1.0
                        try:
                            next(gb)
                        except StopIteration:
                            b_done = True
                if not b_done:
                    for _ in gb:
                        pass

            nsteps_A = {"na": 16, "dl": 56, "mm": 8}
            nsteps_P = {"na": 16, "dl": 16, "mm": 10}
            issue_bt(0)
            issue_bt(1)
            issue_w(0)
            issue_w(1)
            for _ in P_gen(0):
                pass
            for hi in range(nH):
                if hi + 2 < nH:
                    issue_w(hi + 2)
                ga = A_gen(hi)
                gb = P_gen(hi + 1) if hi + 1 < nH else None
                interleave(ga, gb, nsteps_A[heads[hi][0]], nsteps_P[heads[hi + 1][0]] if gb is not None else 1)
                if hi + 2 < nH:
                    issue_bt(hi + 2)

        def phase_C(s, l, par, src, dst):
            ytoks = []
            for gh in range(16):
                ytoks.append(dma(SP, ysem, hT[:, gh, :], ybuf[par, gh * 128:(gh + 1) * 128, :]))
            PF = 4
            groups = [(cb, tt) for cb in range(4) for tt in range(16)]
            base = cnt["xc"]
            cnt["xc"] += len(groups)

            def issue_xload(g):
                cb, tt = groups[g]
                sl = (base + g) % 8
                r0 = s * T + tt * 128
                return dma(SP, xcld[sl], xc[sl], src[r0:r0 + 128, cb * 512:(cb + 1) * 512], deps=xc_res[sl].readers)

            xl = {}
            for g in range(PF):
                xl[g] = issue_xload(g)
            mt = None
            wtok = None
            bigv = None
            s0 = None
            for g, (cb, tt) in enumerate(groups):
                if tt == 0:
                    if s0 is not None:
                        for i in range(4):
                            w_res[s0 + i].readers = [mt]
                    s0 = cnt["w"] % 8
                    assert s0 % 4 == 0
                    cnt["w"] += 4
                    bigv = wbuf[:, s0 * 2048:(s0 + 4) * 2048].rearrange("p (c n) -> p c n", c=NCH)
                    srcw = w_out[l * D:(l + 1) * D, cb * 512:(cb + 1) * 512].rearrange("(c p) n -> p c n", p=128)
                    wtok = dma(POOL, w_sem[s0], bigv, srcw, deps=[w_res[s0 + i].readers for i in range(4)])
                bank, bres = next_pj()
                sl = (base + g) % 8
                r0 = s * T + tt * 128
                PE.wait([wtok] + ytoks + bres.readers)
                for gh in range(16):
                    ins = nc.tensor.matmul(bank[:, :], hT[:, gh, tt * 128:(tt + 1) * 128], bigv[:, gh, :],
                                           start=(gh == 0), stop=(gh == 15))
                mt = PE.op(ins)
                DVE.wait([mt, xl[g]])
                d1 = DVE.op(nc.vector.tensor_tensor(out=xc[sl], in0=bank[:, :], in1=xc[sl], op=ALU.add))
                bres.readers = [d1]
                st_ = dma(SP, xcst[sl], dst[r0:r0 + 128, cb * 512:(cb + 1) * 512], xc[sl], deps=[d1])
                xc_res[sl].readers = [st_]
                if g + PF < len(groups):
                    xl[g + PF] = issue_xload(g + PF)
            for i in range(4):
                w_res[s0 + i].readers = [mt]

        def phase_D(s, src):
            dma(SP, cs_sem, fgbc, fg_in[:, :])
            barrier()
            for tt in range(16):
                k = cnt["xt"]
                cnt["xt"] += 1
                sl = k % 2
                r0 = s * T + tt * 128
                tl = dma(SP, xld[sl], xt[sl], src[r0:r0 + 128, :], deps=xt_res[sl].readers)
                a1, a3, rstd = rstd_chain(xt[sl], xs[sl], [tl] + xs_res[sl].readers)
                DVE.wait([a3])
                d1 = DVE.op(nc.vector.scalar_tensor_tensor(out=xt[sl], in0=xt[sl], scalar=rstd, in1=fgbc,
                                                           op0=ALU.mult, op1=ALU.mult))
                st_ = dma(SP, xst[sl], y_out[r0:r0 + 128, :], xt[sl], deps=[d1])
                xt_res[sl].readers = [st_]
                xs_res[sl].readers = [a1]

        ysem = DSem(ctx, "yld")
        ucount = 0
        for s in range(NSEQ):
            mem_norm(s)
            barrier()
            for li, l in enumerate(layers):
                par = ucount % 2
                ucount += 1
                src = x_in if li == 0 else xbuf
                is_last_layer = (li == len(layers) - 1)
                dst = xbuf if (not is_last_layer or last) else y_out
                phase_norm(src, s * T, 16, l, hT)
                barrier()
                phase_B(s, l, par)
                barrier()
                phase_C(s, l, par, src, dst)
                barrier()
            if last:
                phase_D(s, xbuf)
                barrier()
        barrier()
    return nc


def kernel(x, mem, norm_g, w_in, na_rpb, mem_norm_g, w_mem_kv, w_out, final_g):
    x = np.asarray(x, np.float32)
    mem = np.asarray(mem, np.float32)
    nabt, plan, nt = na_bias_tables(np.asarray(na_rpb, np.float32))
    cst32, cst16 = const_tables()
    gains = np.concatenate([np.asarray(norm_g, np.float32), np.asarray(final_g, np.float32)[None],
                            np.asarray(mem_norm_g, np.float32)[None]], 0)
    gTh = np.ascontiguousarray(gains.reshape(6, NCH, 128).transpose(2, 0, 1)).reshape(128, 96)
    fgbc = np.ascontiguousarray(np.broadcast_to(np.asarray(final_g, np.float32)[None, :], (128, D)))
    w_in2 = np.ascontiguousarray(np.asarray(w_in, np.float32).reshape(DEPTH * D, IN_COLS))
    w_mem2 = np.ascontiguousarray(np.asarray(w_mem_kv, np.float32).reshape(DEPTH * D, 1024))
    w_out2 = np.ascontiguousarray(np.asarray(w_out, np.float32).reshape(DEPTH * D, D))

    cur = [np.ascontiguousarray(x[2 * c:2 * c + 2].reshape(NSEQ * T, D)) for c in range(N_CORES)]
    mems = [np.ascontiguousarray(mem[2 * c:2 * c + 2].reshape(NSEQ * N_MEM, D)) for c in range(N_CORES)]
    groups = [list(range(DEPTH))[i:i + LAYERS_PER_LAUNCH] for i in range(0, DEPTH, LAYERS_PER_LAUNCH)]
    for gi, layers in enumerate(groups):
        nc = build_program(layers, gi == 0, gi == len(groups) - 1, nt * 128, plan)
        in_maps = [{"x": cur[c], "mem": mems[c], "gT": gTh, "fgbc": fgbc, "w_in": w_in2, "w_mem": w_mem2,
                    "w_out": w_out2, "nab": nabt, "cst32": cst32, "cst16": cst16} for c in range(N_CORES)]
        res = run_bass_kernel_spmd(nc, in_maps, core_ids=list(range(N_CORES)))
        cur = [np.asarray(res.results[c]["y"]) for c in range(N_CORES)]
    out = np.stack([cur[c].reshape(NSEQ, T, D) for c in range(N_CORES)], 0).reshape(2 * N_CORES, T, D)
    return out.astype(np.float32, copy=False)
```

```python
import contextlib
import numpy as np
import concourse.bass as bass
import concourse.mybir as mybir
from concourse.bass_utils import run_bass_kernel_spmd

F32 = mybir.dt.float32
BF16 = mybir.dt.bfloat16
AF = mybir.ActivationFunctionType
ALU = mybir.AluOpType

N_CORES = 8
D = 2048
T = 2048
NSEQ = 2
DEPTH = 4
HD = 128
NCH = 16
IN_COLS = 7168
N_MEM = 256
SCALE = float(HD ** -0.5)
EPS = 1e-6
NEG = -30000.0
SEM_LIMIT = 60000
W_OFF = 1408
W_LEN = 2944

LAYERS_PER_LAUNCH = 4


def _rs(r):
    return min(max(r - 4, 0), 24)


def _cs(c):
    return min(max(c - 8, 0), 48)


def na_plan():
    types = {}
    type_list = []
    plan = []
    for qi in range(16):
        lst = []
        for j in range(16):
            valid = []
            for krl in range(2):
                for qrl in range(2):
                    kr = 2 * j + krl
                    qr = 2 * qi + qrl
                    valid.append(_rs(qr) <= kr < _rs(qr) + 8)
            if not any(valid):
                continue
            key = (j - qi, tuple(valid))
            if key not in types:
                types[key] = len(type_list)
                type_list.append((qi, j))
            lst.append((j, types[key]))
        plan.append(lst)
    return plan, type_list


def na_bias_tables(na_rpb):
    plan, type_list = na_plan()
    nt = len(type_list)
    dr_idx = np.zeros((nt, 128, 128), np.int64)
    dc_idx = np.zeros((nt, 128, 128), np.int64)
    ok = np.zeros((nt, 128, 128), bool)
    p = np.arange(128)
    krl, kc = p // 64, p % 64
    qrl, qc = p // 64, p % 64
    cs = np.clip(qc - 8, 0, 48)
    for t, (qi, j) in enumerate(type_list):
        kr = 2 * j + krl[:, None]
        qr = 2 * qi + qrl[None, :]
        rs = np.clip(qr - 4, 0, 24)
        row_ok = (kr >= rs) & (kr < rs + 8)
        col_ok = (kc[:, None] >= cs[None, :]) & (kc[:, None] < cs[None, :] + 16)
        ok[t] = row_ok & col_ok
        dr_idx[t] = np.clip(kr - qr + 7, 0, 14)
        dc_idx[t] = np.clip(kc[:, None] - qc[None, :], -15, 15) + 15
    g = na_rpb[:, :, dr_idx, dc_idx]
    g = np.where(ok[None, None], g, np.float32(NEG)).astype(np.float32)
    g = np.ascontiguousarray(g.transpose(0, 1, 3, 2, 4)).reshape(DEPTH * 6 * 128, nt * 128)
    return g, plan, nt


def const_tables():
    half = HD // 2
    inv = (np.float32(10000.0) ** (-(np.arange(half, dtype=np.float32) / np.float32(half)))).astype(np.float32)
    ang = (np.arange(T, dtype=np.float32)[:, None] * inv[None, :]).astype(np.float32)
    cos = np.cos(ang.astype(np.float64)).astype(np.float32).T
    sin = np.sin(ang.astype(np.float64)).astype(np.float32).T
    cosT = np.concatenate([cos, cos], 0)
    sinS = np.concatenate([-sin, sin], 0)
    swap = np.zeros((128, 128), np.float32)
    for m in range(128):
        swap[(m + 64) % 128, m] = 1.0
    ident = np.eye(128, dtype=np.float32)
    ones = np.ones((128, 128), np.float32)
    p = np.arange(128)[:, None]
    u = np.arange(W_LEN)[None, :]
    d = p - u + W_OFF
    ad = np.abs(d)
    w = (ad <= 64).astype(np.float32) + ((d % 4 == 0) & (ad <= 256)).astype(np.float32) \
        + ((d % 16 == 0) & (ad <= 1024)).astype(np.float32)
    cst32 = np.concatenate([cosT, sinS, swap], 1).astype(np.float32)
    cst16 = np.concatenate([ident, ones, w], 1).astype(np.float32)
    return cst32, cst16


def _flat(deps):
    out = []
    for d in deps:
        if d is None:
            continue
        if isinstance(d, list):
            out.extend(_flat(d))
        else:
            out.append(d)
    return out


class Ctx:
    def __init__(self, nc, stack):
        self.nc = nc
        self.stack = stack
        self.nsem = 0

    def new_sem(self, name):
        self.nsem += 1
        return self.stack.enter_context(self.nc.semaphore(f"{name}_{self.nsem}"))


class Eng:
    def __init__(self, ctx, e, name):
        self.ctx = ctx
        self.e = e
        self.name = name
        self.sem = ctx.new_sem(name)
        self.n = 0
        self.seen = {}
        self.last = None

    def wait(self, deps):
        for (sem, val) in _flat(deps):
            k = id(sem)
            if self.seen.get(k, 0) >= val:
                continue
            self.e.wait_ge(sem, val)
            self.seen[k] = val

    def op(self, ins, sig=True):
        if not sig:
            return None
        if self.n >= SEM_LIMIT:
            self.sem = self.ctx.new_sem(self.name)
            self.n = 0
        self.n += 1
        ins.then_inc(self.sem, 1)
        self.last = (self.sem, self.n)
        return self.last


class DSem:
    def __init__(self, ctx, name):
        self.ctx = ctx
        self.name = name
        self.sem = ctx.new_sem(name)
        self.n = 0

    def inc(self, ins):
        if self.n >= SEM_LIMIT:
            self.sem = self.ctx.new_sem(self.name)
            self.n = 0
        self.n += 16
        ins.then_inc(self.sem, 16)
        return (self.sem, self.n)


def build_program(layers, first, last, nab_cols, plan):
    nc = bass.Bass("TRN2", target_bir_lowering=False)
    NT128 = nab_cols
    x_in = nc.dram_tensor("x", [NSEQ * T, D], F32, kind="ExternalInput").ap()
    mem_in = nc.dram_tensor("mem", [NSEQ * N_MEM, D], F32, kind="ExternalInput").ap()
    gT_in = nc.dram_tensor("gT", [128, 96], F32, kind="ExternalInput").ap()
    fg_in = nc.dram_tensor("fgbc", [128, D], F32, kind="ExternalInput").ap()
    w_in = nc.dram_tensor("w_in", [DEPTH * D, IN_COLS], F32, kind="ExternalInput").ap()
    w_mem = nc.dram_tensor("w_mem", [DEPTH * D, 1024], F32, kind="ExternalInput").ap()
    w_out = nc.dram_tensor("w_out", [DEPTH * D, D], F32, kind="ExternalInput").ap()
    nab = nc.dram_tensor("nab", [DEPTH * 6 * 128, NT128], F32, kind="ExternalInput").ap()
    c32_in = nc.dram_tensor("cst32", [128, 2 * T + 128], F32, kind="ExternalInput").ap()
    c16_in = nc.dram_tensor("cst16", [128, 256 + W_LEN], F32, kind="ExternalInput").ap()
    y_out = nc.dram_tensor("y", [NSEQ * T, D], F32, kind="ExternalOutput").ap()
    xbuf = nc.dram_tensor("xbuf", [NSEQ * T, D], F32).ap()
    ybuf = nc.dram_tensor("ybuf", [2, D, T], BF16).ap()

    sb = nc.alloc_sbuf_tensor
    hT = sb("hT", [128, NCH, T], BF16)
    memnT = sb("memnT", [128, NCH, N_MEM], BF16)
    wbuf = sb("wbuf", [128, 8 * NCH * 128], BF16)
    cosT = sb("cosT", [128, T], F32)
    sinS = sb("sinS", [128, T], F32)
    swapm = sb("swapm", [128, 128], F32)
    c16 = sb("c16", [128, 256 + W_LEN], BF16)
    ident = c16[:, 0:128]
    ones = c16[:, 128:256]
    gT = sb("gTs", [128, 96], F32)
    stats = sb("stats", [128, 640], F32)
    U = sb("U", [128, 10 * T], BF16)
    qTb = [U[:, (0 + i) * T:(1 + i) * T] for i in range(2)]
    kTb = [U[:, (2 + i) * T:(3 + i) * T] for i in range(2)]
    sgb = [U[:, (4 + i) * T:(5 + i) * T] for i in range(2)]
    vb = [U[:, (6 + i) * T:(7 + i) * T] for i in range(2)]
    yTb = [U[:, (8 + i) * T:(9 + i) * T] for i in range(2)]
    xt = [U[:, (2 * i) * T:(2 * i + 2) * T].bitcast(F32) for i in range(2)]
    xs = [U[:, (4 + i) * T:(5 + i) * T] for i in range(2)]
    fgbc = U[:, 6 * T:8 * T].bitcast(F32)
    xc = [U[:, i * 1024:(i + 1) * 1024].bitcast(F32) for i in range(8)]
    PTb = [sb(f"PT{i}", [128, 640], BF16) for i in range(4)]
    Eb = [sb(f"E{i}", [128, 512], BF16) for i in range(3)]
    Btb = [sb(f"Bt{i}", [128, NT128], BF16) for i in range(2)]
    f32tmp = [sb(f"ft{i}", [128, 512], F32) for i in range(10)]
    qfb = [sb(f"qf{i}", [128, 512], F32) for i in range(2)]

    pjb = [nc.alloc_psum_tensor(f"pj{i}", [128, 512], F32) for i in range(2)]
    sbig = [nc.alloc_psum_tensor(f"sS{i}", [128, 1024], F32) for i in range(2)]
    odb = [nc.alloc_psum_tensor(f"od{i}", [128, 512], F32) for i in range(2)]

    with contextlib.ExitStack() as stack:
        ctx = Ctx(nc, stack)
        PE = Eng(ctx, nc.tensor, "pe")
        ACT = Eng(ctx, nc.scalar, "act")
        DVE = Eng(ctx, nc.vector, "dve")
        POOL = Eng(ctx, nc.gpsimd, "pool")
        SP = Eng(ctx, nc.sync, "sp")
        engines = [PE, ACT, DVE, POOL, SP]
        pending_dma = []

        def dma(eng, dsem, out, in_, deps=()):
            eng.wait(deps)
            tok = dsem.inc(eng.e.dma_start(out=out, in_=in_))
            pending_dma.append(tok)
            return tok

        def barrier():
            toks = [e.last for e in engines if e.last is not None] + list(pending_dma)
            for e in engines:
                e.wait(toks)
            del pending_dma[:]

        class Res:
            def __init__(self):
                self.readers = []
                self.writer = None

        cs_sem = DSem(ctx, "cst")
        t_c = [dma(SP, cs_sem, cosT[:], c32_in[:, 0:T]),
               dma(SP, cs_sem, sinS[:], c32_in[:, T:2 * T]),
               dma(SP, cs_sem, swapm[:], c32_in[:, 2 * T:2 * T + 128]),
               dma(SP, cs_sem, gT[:], gT_in[:, :]),
               dma(POOL, cs_sem, c16[:], c16_in[:, :])]
        t_c.append(DVE.op(nc.vector.memset(stats[:], 0.0)))
        barrier()

        stat_col = [0]
        xt_res = [Res(), Res()]
        xs_res = [Res(), Res()]
        pj_res = [Res(), Res()]
        xld = [DSem(ctx, "xld0"), DSem(ctx, "xld1")]
        xst = [DSem(ctx, "xst0"), DSem(ctx, "xst1")]
        xc_res = [Res() for _ in range(8)]
        xcld = [DSem(ctx, f"xcld{i}") for i in range(8)]
        xcst = [DSem(ctx, f"xcst{i}") for i in range(8)]
        cnt = {"qf": 0, "xc": 0, "xt": 0, "pj": 0, "w": 0, "z": 0, "y": 0, "S": 0, "S4": 0, "PT": 0, "E": 0, "od": 0, "ft": 0, "bt": 0}

        def rstd_chain(xtile, junk, deps):
            c = stat_col[0]
            stat_col[0] += 3
            assert stat_col[0] <= 640
            ACT.wait(deps)
            a1 = ACT.op(nc.scalar.activation(out=junk, in_=xtile, func=AF.Square, accum_out=stats[:, c:c + 1]))
            ACT.wait([a1])
            a2 = ACT.op(nc.scalar.activation(out=stats[:, c + 1:c + 2], in_=stats[:, c:c + 1], func=AF.Ln,
                                             scale=1.0 / D, bias=EPS))
            ACT.wait([a2])
            a3 = ACT.op(nc.scalar.activation(out=stats[:, c + 2:c + 3], in_=stats[:, c + 1:c + 2], func=AF.Exp,
                                             scale=-0.5))
            return a1, a3, stats[:, c + 2:c + 3]

        def phase_norm(src, r0, ntiles, gidx, dst):
            for tt in range(ntiles):
                k = cnt["xt"]
                cnt["xt"] += 1
                sl = k % 2
                tl = dma(SP, xld[sl], xt[sl], src[r0 + tt * 128:r0 + (tt + 1) * 128, :], deps=xt_res[sl].readers)
                a1, a3, rstd = rstd_chain(xt[sl], xs[sl], [tl] + xs_res[sl].readers)
                DVE.wait([a3])
                d1 = DVE.op(nc.vector.tensor_scalar(out=xs[sl], in0=xt[sl], scalar1=rstd, scalar2=None,
                                                    op0=ALU.mult))
                xt_res[sl].readers = [a1, d1]
                tps = []
                for half in range(2):
                    kb = cnt["pj"]
                    cnt["pj"] += 1
                    bank = pjb[kb % 2]
                    bres = pj_res[kb % 2]
                    pb = bank[:].bitcast(BF16)
                    PE.wait([d1] + bres.readers)
                    for c8 in range(8):
                        c = half * 8 + c8
                        ins = nc.tensor.transpose(out=pb[:, c8 * 128:(c8 + 1) * 128],
                                                  in_=xs[sl][:, c * 128:(c + 1) * 128], identity=ident)
                    tp = PE.op(ins)
                    tps.append(tp)
                    DVE.wait([tp])
                    gb = gT[:, gidx * 16 + half * 8:gidx * 16 + half * 8 + 8].unsqueeze(2).to_broadcast([128, 8, 128])
                    e1 = DVE.op(nc.vector.tensor_tensor(
                        out=dst[:, half * 8:(half + 1) * 8, tt * 128:(tt + 1) * 128],
                        in0=pb[:, 0:1024].rearrange("p (c t) -> p c t", c=8), in1=gb, op=ALU.mult))
                    bres.readers = [e1]
                xs_res[sl].readers = tps

        w_res = [Res() for _ in range(8)]
        w_sem = [DSem(ctx, f"w{i}") for i in range(8)]

        def wslot_view(s):
            return wbuf[:, s * 2048:(s + 1) * 2048].rearrange("p (c n) -> p c n", c=NCH)

        def load_wblock(wt, layer, c0):
            s = cnt["w"] % 8
            cnt["w"] += 1
            src = wt[layer * D:(layer + 1) * D, c0:c0 + 128].rearrange("(c p) n -> p c n", p=128)
            tok = dma(POOL, w_sem[s], wslot_view(s), src, deps=w_res[s].readers)
            w_res[s].writer = tok
            return s, tok

        z_res = [Res(), Res()]
        y_res = [Res(), Res()]
        yst = [DSem(ctx, "yst0"), DSem(ctx, "yst1")]
        S_res = [Res(), Res()]
        S4_res = [Res() for _ in range(4)]
        PT_res = [Res() for _ in range(4)]
        E_res = [Res() for _ in range(3)]
        od_res = [Res(), Res()]
        ft_res = [Res() for _ in range(10)]
        bt_res = [Res(), Res()]
        bt_sem = [DSem(ctx, "bt0"), DSem(ctx, "bt1")]

        def S4_view(i):
            return sbig[i // 2][:, (i % 2) * 512:(i % 2 + 1) * 512]

        def next_ft():
            k = cnt["ft"] % 10
            cnt["ft"] += 1
            return f32tmp[k], ft_res[k]

        def next_pj():
            k = cnt["pj"] % 2
            cnt["pj"] += 1
            return pjb[k], pj_res[k]

        def proj_fm(ws, wtok, rhs_of, ncols, evac, zdeps):
            wv = wslot_view(ws)
            nb_ = (T if ncols == 512 else N_MEM) // ncols
            mt = None
            for tb in range(nb_):
                bank, bres = next_pj()
                PE.wait([wtok] + bres.readers)
                for c in range(NCH):
                    ins = nc.tensor.matmul(bank[:, 0:ncols], wv[:, c, :], rhs_of(c, tb), start=(c == 0),
                                           stop=(c == NCH - 1))
                mt = PE.op(ins)
                bres.readers = _flat([evac(tb, bank, mt)])
                w_res[ws].readers = [mt]
                yield

        def proj_tm(ws, wtok, lhs_of, ntiles, dst, zdeps):
            wv = wslot_view(ws)
            mt = None
            ngrp = (ntiles + 3) // 4
            for g in range(ngrp):
                bank, bres = next_pj()
                nt_ = min(4, ntiles - g * 4)
                PE.wait([wtok] + bres.readers)
                for t4 in range(nt_):
                    tt = g * 4 + t4
                    for c in range(NCH):
                        ins = nc.tensor.matmul(bank[:, t4 * 128:(t4 + 1) * 128], lhs_of(c, tt), wv[:, c, :],
                                               start=(c == 0), stop=(c == NCH - 1))
                mt = PE.op(ins)
                ACT.wait([mt])
                ev = ACT.op(nc.scalar.copy(out=dst[:, g * 512:g * 512 + nt_ * 128], in_=bank[:, 0:nt_ * 128]))
                bres.readers = [ev]
                w_res[ws].readers = [mt]
                yield

        def hT_rhs(c, tb):
            return hT[:, c, tb * 512:(tb + 1) * 512]

        def evac_copy(dst, scale):
            def f(tb, bank, tok):
                ACT.wait([tok])
                n = 512
                return ACT.op(nc.scalar.activation(out=dst[:, tb * n:(tb + 1) * n], in_=bank[:, 0:n], func=AF.Copy,
                                                   scale=scale))
            return f

        rope_pending = []
        qf_res = [Res() for _ in range(2)]

        def flush_rope():
            while rope_pending:
                rope_pending.pop(0)()

        def evac_rope(dst, scale):
            def f(tb, bank, tok):
                flush_rope()
                kq = cnt["qf"] % 2
                cnt["qf"] += 1
                qf, qres = qfb[kq], qf_res[kq]
                ACT.wait([tok] + qres.readers)
                a = ACT.op(nc.scalar.activation(out=qf[:], in_=bank[:, 0:512], func=AF.Copy, scale=scale))
                qres.readers = [a]

                def rest():
                    k = cnt["S4"] % 4
                    cnt["S4"] += 1
                    rp, rres = S4_view(k), S4_res[k]
                    PE.wait([a] + rres.readers)
                    r = PE.op(nc.tensor.matmul(rp, swapm[:], qf[:], start=True, stop=True))
                    t1, t1res = next_ft()
                    DVE.wait([a] + t1res.readers)
                    d1 = DVE.op(nc.vector.tensor_tensor(out=t1[:], in0=qf[:], in1=cosT[:, tb * 512:(tb + 1) * 512],
                                                        op=ALU.mult))
                    t2, t2res = next_ft()
                    DVE.wait([r] + t2res.readers)
                    d2 = DVE.op(nc.vector.tensor_tensor(out=t2[:], in0=rp, in1=sinS[:, tb * 512:(tb + 1) * 512],
                                                        op=ALU.mult))
                    rres.readers = [d2]
                    DVE.wait([d1, d2])
                    d3 = DVE.op(nc.vector.tensor_tensor(out=dst[:, tb * 512:(tb + 1) * 512], in0=t1[:], in1=t2[:],
                                                        op=ALU.add))
                    qres.readers = [r, d1]
                    t1res.readers = [d3]
                    t2res.readers = [d3]
                rope_pending.append(rest)
                return a
            return f

        def evac_silu(dst):
            def f(tb, bank, tok):
                te, teres = next_ft()
                t2, t2res = next_ft()
                ACT.wait([tok] + teres.readers + t2res.readers)
                a1 = ACT.op(nc.scalar.activation(out=te[:], in_=bank[:, 0:512], func=AF.Exp, scale=-1.0))
                ACT.wait([a1])
                a2 = ACT.op(nc.scalar.activation(out=t2[:], in_=te[:], func=AF.Ln, bias=1.0))
                ACT.wait([a2])
                a3 = ACT.op(nc.scalar.activation(out=te[:], in_=t2[:], func=AF.Exp, scale=-1.0))
                DVE.wait([a3])
                d3 = DVE.op(nc.vector.tensor_tensor(out=dst[:, tb * 512:(tb + 1) * 512], in0=bank[:, 0:512],
                                                    in1=te[:], op=ALU.mult))
                teres.readers = [d3]
                t2res.readers = [a3]
                return [d3]
            return f

        def normalize(ptok, o_ap, d_ap, n, ydst, sg_ap):
            te, teres = next_ft()
            tr, trres = next_ft()
            t2, t2res = next_ft()
            tq, tres = next_ft()
            DVE.wait([ptok] + teres.readers)
            c0 = DVE.op(nc.vector.tensor_scalar(out=te[:, 0:n], in0=d_ap, scalar1=1.0, scalar2=None, op0=ALU.mult))
            DVE.wait(tres.readers)
            c1 = DVE.op(nc.vector.tensor_tensor(out=tq[:, 0:n], in0=o_ap, in1=sg_ap, op=ALU.mult))
            ACT.wait([c0] + trres.readers + t2res.readers)
            a1 = ACT.op(nc.scalar.activation(out=tr[:, 0:n], in_=te[:, 0:n], func=AF.Ln))
            ACT.wait([a1])
            a2 = ACT.op(nc.scalar.activation(out=t2[:, 0:n], in_=tr[:, 0:n], func=AF.Exp, scale=-1.0))
            DVE.wait([a2, c1])
            d3 = DVE.op(nc.vector.tensor_tensor(out=ydst, in0=tq[:, 0:n], in1=t2[:, 0:n], op=ALU.mult))
            teres.readers = [a1]
            trres.readers = [a2]
            t2res.readers = [d3]
            tres.readers = [d3]
            return d3, c1, c0

        ystore = {}
        pre = {}

        def store_y(yp, par, gh):
            tok = dma(SP, yst[yp], ybuf[par, gh * 128:(gh + 1) * 128, :], yTb[yp], deps=[DVE.last])
            y_res[yp].readers = [tok]
            ystore[gh] = tok

        def issue_wout(l, cb):
            s0 = cnt["w"] % 8
            assert s0 % 4 == 0
            cnt["w"] += 4
            bigv = wbuf[:, s0 * 2048:(s0 + 4) * 2048].rearrange("p (c n) -> p c n", c=NCH)
            srcw = w_out[l * D:(l + 1) * D, cb * 512:(cb + 1) * 512].rearrange("(c p) n -> p c n", p=128)
            wtok = dma(POOL, w_sem[s0], bigv, srcw, deps=[w_res[s0 + i].readers for i in range(4)])
            return s0, wtok, bigv

        def attn_na(zp, yp, btp, bt_tok):
            qT, kT, sg, v, yT = qTb[zp], kTb[zp], sgb[zp], vb[zp], yTb[yp]
            Bt = Btb[btp]
            prev = None
            last_pe = None

            def pv(st):
                qi, J, pk, et = st
                ko = cnt["od"] % 2
                cnt["od"] += 1
                od, ores = odb[ko], od_res[ko]
                PE.wait([et] + ores.readers)
                nj = len(J)
                for jj, (j, ty) in enumerate(J):
                    nc.tensor.matmul(od[:, 0:128], v[:, j * 128:(j + 1) * 128], PTb[pk][:, jj * 128:(jj + 1) * 128],
                                     start=(jj == 0), stop=(jj == nj - 1))
                for jj, (j, ty) in enumerate(J):
                    ins = nc.tensor.matmul(od[:, 128:256], ones, PTb[pk][:, jj * 128:(jj + 1) * 128],
                                           start=(jj == 0), stop=(jj == nj - 1))
                pt = PE.op(ins)
                PT_res[pk].readers = [pt]
                d3, c1, a1 = normalize(pt, od[:, 0:128], od[:, 128:256], 128, yT[:, qi * 128:(qi + 1) * 128],
                                       sg[:, qi * 128:(qi + 1) * 128])
                ores.readers = [c1, a1]
                return pt

            for qi in range(16):
                J = plan[qi]
                ks = cnt["S"] % 2
                cnt["S"] += 1
                sbk, sres = sbig[ks], S_res[ks]
                PE.wait(sres.readers + S4_res[2 * ks].readers + S4_res[2 * ks + 1].readers + [bt_tok])
                for jj, (j, ty) in enumerate(J):
                    nc.tensor.matmul(sbk[:, jj * 128:(jj + 1) * 128], kT[:, j * 128:(j + 1) * 128],
                                     qT[:, qi * 128:(qi + 1) * 128], start=True, stop=False)
                    ins = nc.tensor.matmul(sbk[:, jj * 128:(jj + 1) * 128], ident, Bt[:, ty * 128:(ty + 1) * 128],
                                           start=False, stop=True)
                st = PE.op(ins)
                pk = cnt["PT"] % 4
                cnt["PT"] += 1
                n = len(J) * 128
                ACT.wait([st] + PT_res[pk].readers)
                n1 = min(n, 512)
                et = ACT.op(nc.scalar.activation(out=PTb[pk][:, 0:n1], in_=sbk[:, 0:n1], func=AF.Exp))
                if n > 512:
                    et = ACT.op(nc.scalar.activation(out=PTb[pk][:, 512:n], in_=sbk[:, 512:n], func=AF.Exp))
                sres.readers = [et]
                S4_res[2 * ks].readers = [et]
                S4_res[2 * ks + 1].readers = [et]
                if prev is not None:
                    last_pe = pv(prev)
                prev = (qi, J, pk, et)
                yield
            last_pe = pv(prev)
            z_res[zp].readers = [last_pe, DVE.last]
            bt_res[btp].readers = [last_pe]

        def attn_blocks(zp, yp, items, masked):
            qT, sg, yT = qTb[zp], sgb[zp], yTb[yp]
            LA = 3
            staged = []
            last_pe = [None]

            def pv(st):
                b, vap, pk, mtok, first, lastf = st
                if first:
                    PE.wait(od_res[0].readers + od_res[1].readers)
                PE.wait([mtok])
                nc.tensor.matmul(odb[0][:, :], vap, PTb[pk][:, 0:512], start=first, stop=lastf)
                pt = PE.op(nc.tensor.matmul(odb[1][:, :], ones, PTb[pk][:, 0:512], start=first, stop=lastf))
                PT_res[pk].readers = [pt]
                last_pe[0] = pt
                if lastf:
                    d3, c1, a1 = normalize(pt, odb[0][:, :], odb[1][:, :], 512, yT[:, b * 512:(b + 1) * 512],
                                           sg[:, b * 512:(b + 1) * 512])
                    od_res[0].readers = [c1]
                    od_res[1].readers = [a1]

            for (b, kap, vap, moff, first, lastf) in items:
                k4 = cnt["S4"] % 4
                cnt["S4"] += 1
                sv_, sres = S4_view(k4), S4_res[k4]
                PE.wait(sres.readers)
                st = PE.op(nc.tensor.matmul(sv_, kap, qT[:, b * 512:(b + 1) * 512], start=True, stop=True))
                pk = cnt["PT"] % 4
                cnt["PT"] += 1
                if masked:
                    ke = cnt["E"] % 3
                    cnt["E"] += 1
                    ACT.wait([st] + E_res[ke].readers)
                    et = ACT.op(nc.scalar.activation(out=Eb[ke][:], in_=sv_, func=AF.Exp))
                    sres.readers = [et]
                    DVE.wait([et] + PT_res[pk].readers)
                    mtok = DVE.op(nc.vector.tensor_tensor(out=PTb[pk][:, 0:512], in0=Eb[ke][:],
                                                          in1=c16[:, 256 + moff:256 + moff + 512], op=ALU.mult))
                    E_res[ke].readers = [mtok]
                else:
                    ACT.wait([st] + PT_res[pk].readers)
                    mtok = ACT.op(nc.scalar.activation(out=PTb[pk][:, 0:512], in_=sv_, func=AF.Exp))
                    sres.readers = [mtok]
                staged.append((b, vap, pk, mtok, first, lastf))
                if len(staged) > LA:
                    pv(staged.pop(0))
                yield
            while staged:
                pv(staged.pop(0))
            z_res[zp].readers = [last_pe[0], DVE.last]

        def mem_norm(s):
            phase_norm(mem_in, s * N_MEM, 2, 5, memnT)

        def phase_B(s, l, par):
            heads = [("na", h) for h in range(6)] + [("dl", h) for h in range(6)] + [("mm", h) for h in range(4)]
            def wblocks(kind, h):
                if kind == "na":
                    return [(w_in, 0 + h * 128), (w_in, 768 + h * 128), (w_in, 1536 + h * 128), (w_in, 2304 + h * 128)]
                if kind == "dl":
                    return [(w_in, 3072 + h * 128), (w_in, 3840 + h * 128), (w_in, 4608 + h * 128),
                            (w_in, 5376 + h * 128)]
                return [(w_in, 6144 + h * 128), (w_mem, h * 128), (w_mem, 512 + h * 128), (w_in, 6656 + h * 128)]

            def issue_loads(idx):
                kind, h = heads[idx]
                res = {}
                if kind == "na":
                    kb = cnt["bt"] % 2
                    cnt["bt"] += 1
                    r0 = (l * 6 + h) * 128
                    res["bt"] = (kb, dma(POOL, bt_sem[kb], Btb[kb][:], nab[r0:r0 + 128, :], deps=bt_res[kb].readers))
                res["w"] = [load_wblock(wt, l, c0) for (wt, c0) in wblocks(kind, h)]
                return res

            nH = len(heads)
            loads = {}

            def issue_w(idx):
                kind, h = heads[idx]
                loads.setdefault(idx, {})["w"] = [load_wblock(wt, l, c0) for (wt, c0) in wblocks(kind, h)]

            def issue_bt(idx):
                kind, h = heads[idx]
                if kind != "na":
                    return
                kb = cnt["bt"] % 2
                cnt["bt"] += 1
                r0 = (l * 6 + h) * 128
                loads.setdefault(idx, {})["bt"] = (kb, dma(POOL, bt_sem[kb], Btb[kb][:], nab[r0:r0 + 128, :],
                                                          deps=bt_res[kb].readers))

            def P_gen(hi):
                kind, h = heads[hi]
                cur = loads[hi]
                zp = hi % 2
                zdeps = list(z_res[zp].readers)
                ACT.wait(zdeps)
                DVE.wait(zdeps)
                (wq, tq), (wk, tk), (wvv, tv), (wg, tg) = cur["w"]
                rope = kind == "dl"
                yield from proj_fm(wq, tq, hT_rhs, 512, (evac_rope if rope else evac_copy)(qTb[zp], SCALE), zdeps)
                if kind == "mm":
                    def ev_mk(tb, bank, tok):
                        ACT.wait([tok])
                        return ACT.op(nc.scalar.copy(out=kTb[zp][:, 0:N_MEM], in_=bank[:, 0:N_MEM]))
                    yield from proj_fm(wk, tk, lambda c, tb: memnT[:, c, :], N_MEM, ev_mk, zdeps)
                    yield from proj_tm(wvv, tv, lambda c, tt: memnT[:, c, tt * 128:(tt + 1) * 128], 2, vb[zp], zdeps)
                else:
                    yield from proj_fm(wk, tk, hT_rhs, 512, (evac_rope if rope else evac_copy)(kTb[zp], 1.0), zdeps)
                    yield from proj_tm(wvv, tv, lambda c, tt: hT[:, c, tt * 128:(tt + 1) * 128], 16, vb[zp], zdeps)
                flush_rope()
                yield from proj_fm(wg, tg, hT_rhs, 512, evac_silu(sgb[zp]), zdeps)

            def A_gen(hi):
                kind, h = heads[hi]
                cur = loads[hi]
                zp = hi % 2
                yp = hi % 2
                PE.wait([ACT.last, DVE.last] + y_res[yp].readers)
                DVE.wait(y_res[yp].readers)
                if kind == "na":
                    yield from attn_na(zp, yp, cur["bt"][0], cur["bt"][1])
                elif kind == "dl":
                    items = []
                    for b in range(4):
                        js = list(range(max(0, 4 * b - 8), min(15, 4 * b + 11) + 1))
                        for j in js:
                            items.append((b, kTb[zp][:, j * 128:(j + 1) * 128], vb[zp][:, j * 128:(j + 1) * 128],
                                          W_OFF - 128 * j + 512 * b, j == js[0], j == js[-1]))
                    yield from attn_blocks(zp, yp, items, True)
                else:
                    items = []
                    for b in range(4):
                        for mt_ in range(2):
                            items.append((b, kTb[zp][:, mt_ * 128:(mt_ + 1) * 128],
                                          vb[zp][:, mt_ * 128:(mt_ + 1) * 128], None, mt_ == 0, mt_ == 1))
                    yield from attn_blocks(zp, yp, items, False)
                gh = {"na": 0, "dl": 6, "mm": 12}[kind] + h
                store_y(yp, par, gh)

            def interleave(ga, gb, na, nb):
                ratio = (nb / na) if gb is not None else 0.0
                acc = 0.0
                b_done = gb is None
                if not b_done:
                    try:
                        next(gb)
                    except StopIteration:
                        b_done = True
                for _ in ga:
                    acc += ratio
                    while acc >= 1.0 and not b_done:
                        acc -= 1.0
                        try:
                            next(gb)
                        except StopIteration:
                            b_done = True
                if not b_done:
                    for _ in gb:
                        pass

            nsteps_A = {"na": 16, "dl": 56, "mm": 8}
            nsteps_P = {"na": 16, "dl": 16, "mm": 10}
            issue_bt(0)
            issue_bt(1)
            issue_w(0)
            issue_w(1)
            for _ in P_gen(0):
                pass
            for hi in range(nH):
                if hi + 2 < nH:
                    issue_w(hi + 2)
                if hi == nH - 1:
                    pre["ytoks"] = [dma(SP, ysem, hT[:, gh, :], ybuf[par, gh * 128:(gh + 1) * 128, :],
                                        deps=[PE.last, ystore[gh]]) for gh in range(nH - 1)]
                    pre["w0"] = issue_wout(l, 0)
                ga = A_gen(hi)
                gb = P_gen(hi + 1) if hi + 1 < nH else None
                interleave(ga, gb, nsteps_A[heads[hi][0]], nsteps_P[heads[hi + 1][0]] if gb is not None else 1)
                if hi + 2 < nH:
                    issue_bt(hi + 2)

        def phase_C(s, l, par, src, dst):
            ytoks = list(pre["ytoks"])
            for gh in range(len(ytoks), 16):
                ytoks.append(dma(SP, ysem, hT[:, gh, :], ybuf[par, gh * 128:(gh + 1) * 128, :]))
            PF = 4
            groups = [(cb, tt) for cb in range(4) for tt in range(16)]
            base = cnt["xc"]
            cnt["xc"] += len(groups)

            def issue_xload(g):
                cb, tt = groups[g]
                sl = (base + g) % 8
                r0 = s * T + tt * 128
                return dma(SP, xcld[sl], xc[sl], src[r0:r0 + 128, cb * 512:(cb + 1) * 512], deps=xc_res[sl].readers)

            xl = {}
            for g in range(PF):
                xl[g] = issue_xload(g)
            mt = None
            wtok = None
            bigv = None
            s0 = None
            for g, (cb, tt) in enumerate(groups):
                if tt == 0:
                    if s0 is not None:
                        for i in range(4):
                            w_res[s0 + i].readers = [mt]
                    s0, wtok, bigv = pre["w0"] if cb == 0 else issue_wout(l, cb)
                bank, bres = next_pj()
                sl = (base + g) % 8
                r0 = s * T + tt * 128
                PE.wait([wtok] + ytoks + bres.readers)
                for gh in range(16):
                    ins = nc.tensor.matmul(bank[:, :], hT[:, gh, tt * 128:(tt + 1) * 128], bigv[:, gh, :],
                                           start=(gh == 0), stop=(gh == 15))
                mt = PE.op(ins)
                DVE.wait([mt, xl[g]])
                d1 = DVE.op(nc.vector.tensor_tensor(out=xc[sl], in0=bank[:, :], in1=xc[sl], op=ALU.add))
                bres.readers = [d1]
                st_ = dma(SP, xcst[sl], dst[r0:r0 + 128, cb * 512:(cb + 1) * 512], xc[sl], deps=[d1])
                xc_res[sl].readers = [st_]
                if g + PF < len(groups):
                    xl[g + PF] = issue_xload(g + PF)
            for i in range(4):
                w_res[s0 + i].readers = [mt]

        def phase_D(s, src):
            dma(SP, cs_sem, fgbc, fg_in[:, :])
            barrier()
            for tt in range(16):
                k = cnt["xt"]
                cnt["xt"] += 1
                sl = k % 2
                r0 = s * T + tt * 128
                tl = dma(SP, xld[sl], xt[sl], src[r0:r0 + 128, :], deps=xt_res[sl].readers)
                a1, a3, rstd = rstd_chain(xt[sl], xs[sl], [tl] + xs_res[sl].readers)
                DVE.wait([a3])
                d1 = DVE.op(nc.vector.scalar_tensor_tensor(out=xt[sl], in0=xt[sl], scalar=rstd, in1=fgbc,
                                                           op0=ALU.mult, op1=ALU.mult))
                st_ = dma(SP, xst[sl], y_out[r0:r0 + 128, :], xt[sl], deps=[d1])
                xt_res[sl].readers = [st_]
                xs_res[sl].readers = [a1]

        ysem = DSem(ctx, "yld")
        ucount = 0
        for s in range(NSEQ):
            mem_norm(s)
            barrier()
            for li, l in enumerate(layers):
                par = ucount % 2
                ucount += 1
                src = x_in if li == 0 else xbuf
                is_last_layer = (li == len(layers) - 1)
                dst = xbuf if (not is_last_layer or last) else y_out
                phase_norm(src, s * T, 16, l, hT)
                barrier()
                phase_B(s, l, par)
                barrier()
                phase_C(s, l, par, src, dst)
                barrier()
            if last:
                phase_D(s, xbuf)
                barrier()
        barrier()
    return nc


def kernel(x, mem, norm_g, w_in, na_rpb, mem_norm_g, w_mem_kv, w_out, final_g):
    x = np.asarray(x, np.float32)
    mem = np.asarray(mem, np.float32)
    nabt, plan, nt = na_bias_tables(np.asarray(na_rpb, np.float32))
    cst32, cst16 = const_tables()
    gains = np.concatenate([np.asarray(norm_g, np.float32), np.asarray(final_g, np.float32)[None],
                            np.asarray(mem_norm_g, np.float32)[None]], 0)
    gTh = np.ascontiguousarray(gains.reshape(6, NCH, 128).transpose(2, 0, 1)).reshape(128, 96)
    fgbc = np.ascontiguousarray(np.broadcast_to(np.asarray(final_g, np.float32)[None, :], (128, D)))
    w_in2 = np.ascontiguousarray(np.asarray(w_in, np.float32).reshape(DEPTH * D, IN_COLS))
    w_mem2 = np.ascontiguousarray(np.asarray(w_mem_kv, np.float32).reshape(DEPTH * D, 1024))
    w_out2 = np.ascontiguousarray(np.asarray(w_out, np.float32).reshape(DEPTH * D, D))

    cur = [np.ascontiguousarray(x[2 * c:2 * c + 2].reshape(NSEQ * T, D)) for c in range(N_CORES)]
    mems = [np.ascontiguousarray(mem[2 * c:2 * c + 2].reshape(NSEQ * N_MEM, D)) for c in range(N_CORES)]
    groups = [list(range(DEPTH))[i:i + LAYERS_PER_LAUNCH] for i in range(0, DEPTH, LAYERS_PER_LAUNCH)]
    for gi, layers in enumerate(groups):
        nc = build_program(layers, gi == 0, gi == len(groups) - 1, nt * 128, plan)
        in_maps = [{"x": cur[c], "mem": mems[c], "gT": gTh, "fgbc": fgbc, "w_in": w_in2, "w_mem": w_mem2,
                    "w_out": w_out2, "nab": nabt, "cst32": cst32, "cst16": cst16} for c in range(N_CORES)]
        res = run_bass_kernel_spmd(nc, in_maps, core_ids=list(range(N_CORES)))
        cur = [np.asarray(res.results[c]["y"]) for c in range(N_CORES)]
    out = np.stack([cur[c].reshape(NSEQ, T, D) for c in range(N_CORES)], 0).reshape(2 * N_CORES, T, D)
    return out.astype(np.float32, copy=False)
```
